# Optimizing a Trainium2 kernel written in Bass

```python
import jax, jax.numpy as jnp
from jax import lax
import numpy as np

D_MODEL = 4096
BATCH = 2
SEQ = 8192
DEPTH = 4

HEAD_DIM = 64
N_Q_HEADS = D_MODEL // 128
KV_RATIO = 8
N_KV_HEADS = N_Q_HEADS // KV_RATIO
ATTN_WIDTH = N_Q_HEADS * HEAD_DIM
KV_WIDTH = N_KV_HEADS * HEAD_DIM
CONV_WIDTH = D_MODEL // 4
CONV_K = 3
WINDOW = 128
BLOCK = 128
ROPE_THETA = 500000.0
ROT_DIM = HEAD_DIM // 4
D_FF = 4 * D_MODEL
ALPHA = (2 * DEPTH) ** 0.25
BETA = (8 * DEPTH) ** -0.25
LN_EPS = 1e-5
NEG_INF = -1e30
IN_WIDTHS = (ATTN_WIDTH, KV_WIDTH, KV_WIDTH, CONV_WIDTH, CONV_WIDTH, CONV_WIDTH, D_MODEL, D_MODEL)
IN_WIDTH = ATTN_WIDTH + 2 * KV_WIDTH + 3 * CONV_WIDTH + 2 * D_MODEL

kernel_name = "hybrid_swa_sink_shortconv_gated_deepnorm"


def layer_norm(x, g, b):
    xf = x.astype(jnp.float32)
    mu = jnp.mean(xf, axis=-1, keepdims=True)
    var = jnp.mean(jnp.square(xf - mu), axis=-1, keepdims=True)
    y = (xf - mu) * lax.rsqrt(var + LN_EPS) * g.astype(jnp.float32) + b.astype(jnp.float32)
    return y.astype(x.dtype)


def rope_tables(positions):
    inv_freq = ROPE_THETA ** (-jnp.arange(0, ROT_DIM, 2, dtype=jnp.float32) / ROT_DIM)
    ang = positions.astype(jnp.float32)[..., None] * inv_freq
    return jnp.cos(ang)[:, :, None, :], jnp.sin(ang)[:, :, None, :]


def apply_partial_rope(t, cos, sin):
    half = ROT_DIM // 2
    rot = t[..., :ROT_DIM].astype(jnp.float32)
    t1, t2 = rot[..., :half], rot[..., half:]
    rotated = jnp.concatenate([t1 * cos - t2 * sin, t2 * cos + t1 * sin], axis=-1)
    return jnp.concatenate([rotated.astype(t.dtype), t[..., ROT_DIM:]], axis=-1)


def sliding_window_sink_attention(q, k, v, sinks):
    b, s = q.shape[0], q.shape[1]
    nblk = s // BLOCK
    qb = q.reshape(b, nblk, BLOCK, N_KV_HEADS, KV_RATIO, HEAD_DIM)

    def band(t):
        tb = t.reshape(b, nblk, BLOCK, N_KV_HEADS, HEAD_DIM)
        prev = jnp.pad(tb, ((0, 0), (1, 0), (0, 0), (0, 0), (0, 0)))[:, :-1]
        return jnp.concatenate([prev, tb], axis=2)

    kb, vb = band(k), band(v)
    scores = jnp.einsum('bnqhgd,bnkhd->bnhgqk', qb, kb).astype(jnp.float32) * (HEAD_DIM ** -0.5)
    qi = jnp.arange(BLOCK)[:, None]
    kj = jnp.arange(2 * BLOCK)[None, :]
    rel = qi + BLOCK - kj
    in_window = (rel >= 0) & (rel < WINDOW)
    real_key = (jnp.arange(nblk)[:, None] > 0) | (kj >= BLOCK)
    mask = in_window[None] & real_key[:, None, :]
    scores = jnp.where(mask[None, :, None, None], scores, NEG_INF)
    sink = sinks.astype(jnp.float32).reshape(1, 1, N_KV_HEADS, KV_RATIO, 1, 1)
    sink = jnp.broadcast_to(sink, scores.shape[:-1] + (1,))
    probs = jax.nn.softmax(jnp.concatenate([scores, sink], axis=-1), axis=-1)[..., :-1]
    out = jnp.einsum('bnhgqk,bnkhd->bnqhgd', probs.astype(v.dtype), vb)
    return out.reshape(b, s, ATTN_WIDTH)


def gated_short_conv(b_gate, c_gate, h, w):
    s = h.shape[1]
    u = c_gate * h
    u_pad = jnp.pad(u, ((0, 0), (CONV_K - 1, 0), (0, 0)))
    conv = w[0] * u_pad[:, 0:s]
    for tap in range(1, CONV_K):
        conv = conv + w[tap] * u_pad[:, tap:tap + s]
    return b_gate * conv


def setup_inputs(seed: int = 0) -> dict:
    key = jax.random.key(seed)
    ks = jax.random.split(key, 16)
    f32 = jnp.float32
    x = jax.random.normal(ks[0], (BATCH, SEQ, D_MODEL), f32)
    offset = jax.random.randint(ks[1], (BATCH, 1), 0, 4096, dtype=jnp.int32)
    positions = (offset + jnp.arange(SEQ, dtype=jnp.int32)[None, :]).astype(jnp.int32)
    v0 = ATTN_WIDTH + KV_WIDTH
    h0 = ATTN_WIDTH + 2 * KV_WIDTH + 2 * CONV_WIDTH
    col_scale = jnp.ones((IN_WIDTH,), f32).at[v0:v0 + KV_WIDTH].set(BETA).at[h0:h0 + CONV_WIDTH].set(BETA)
    w_in = jax.random.normal(ks[2], (DEPTH, D_MODEL, IN_WIDTH), f32) * (D_MODEL ** -0.5) * col_scale
    conv_w = jax.random.normal(ks[3], (DEPTH, CONV_K, CONV_WIDTH), f32) * (CONV_K ** -0.5)
    attn_sinks = jax.random.normal(ks[4], (DEPTH, N_Q_HEADS), f32) * 0.5
    w_br_attn = jax.random.normal(ks[5], (DEPTH, ATTN_WIDTH, D_MODEL), f32) * (ATTN_WIDTH ** -0.5) * BETA
    w_br_conv = jax.random.normal(ks[6], (DEPTH, CONV_WIDTH, D_MODEL), f32) * (CONV_WIDTH ** -0.5) * BETA
    w_o = jax.random.normal(ks[7], (DEPTH, D_MODEL, D_MODEL), f32) * (D_MODEL ** -0.5) * BETA
    ln1_g = 1.0 + 0.02 * jax.random.normal(ks[8], (DEPTH, D_MODEL), f32)
    ln1_b = 0.02 * jax.random.normal(ks[9], (DEPTH, D_MODEL), f32)
    w_up = jax.random.normal(ks[10], (DEPTH, D_MODEL, D_FF), f32) * (D_MODEL ** -0.5)
    w_down = jax.random.normal(ks[11], (DEPTH, D_FF, D_MODEL), f32) * (D_FF ** -0.5) * BETA
    ln2_g = 1.0 + 0.02 * jax.random.normal(ks[12], (DEPTH, D_MODEL), f32)
    ln2_b = 0.02 * jax.random.normal(ks[13], (DEPTH, D_MODEL), f32)
    return {"x": x, "positions": positions, "w_in": w_in, "conv_w": conv_w,
            "attn_sinks": attn_sinks, "w_br_attn": w_br_attn, "w_br_conv": w_br_conv,
            "w_o": w_o, "ln1_g": ln1_g, "ln1_b": ln1_b, "w_up": w_up, "w_down": w_down,
            "ln2_g": ln2_g, "ln2_b": ln2_b}


def reference(x, positions, w_in, conv_w, attn_sinks, w_br_attn, w_br_conv, w_o,
              ln1_g, ln1_b, w_up, w_down, ln2_g, ln2_b):
    b, s = x.shape[0], x.shape[1]
    cos, sin = rope_tables(positions)
    split_points = [int(p) for p in np.cumsum(IN_WIDTHS)[:-1]]
    for l in range(DEPTH):
        proj = x @ w_in[l]
        q, k, v, cb, cc, ch, ga, gc = jnp.split(proj, split_points, axis=-1)
        q = apply_partial_rope(q.reshape(b, s, N_Q_HEADS, HEAD_DIM), cos, sin)
        k = apply_partial_rope(k.reshape(b, s, N_KV_HEADS, HEAD_DIM), cos, sin)
        v = v.reshape(b, s, N_KV_HEADS, HEAD_DIM)
        attn = sliding_window_sink_attention(q, k, v, attn_sinks[l])
        conv = gated_short_conv(cb, cc, ch, conv_w[l])
        merged = jax.nn.sigmoid(ga) * (attn @ w_br_attn[l]) + jax.nn.sigmoid(gc) * (conv @ w_br_conv[l])
        x = layer_norm(ALPHA * x + merged @ w_o[l], ln1_g[l], ln1_b[l])
        hidden = jnp.square(jax.nn.relu(x @ w_up[l]))
        x = layer_norm(ALPHA * x + hidden @ w_down[l], ln2_g[l], ln2_b[l])
    return x
```

```python
import contextlib
import math
import numpy as np
import concourse.bass as bass
import concourse.mybir as mybir
from concourse.bass_utils import run_bass_kernel_spmd

F32 = mybir.dt.float32
BF16 = mybir.dt.bfloat16
I32 = mybir.dt.int32
AF = mybir.ActivationFunctionType
ALU = mybir.AluOpType

D = 4096
NCH = 32
DEPTH = 4
NCORES = 8
TOK_CORE = 2048
HALO = 128
ALPHA = (2 * DEPTH) ** 0.25
LN_EPS = 1e-5
ROPE_THETA = 500000.0
TWO_PI = 2.0 * math.pi
NW = 5
SAME_ENGINE_SYNC = True
FUSED = True

N_WIN = 112
N_WBR = 32
N_WO = 32
N_WUP = 128
N_WDN = 128


class Op:
    __slots__ = ("eng", "fn", "deps", "marked", "count", "sig", "dma")

    def __init__(self, eng, fn, sig, dma):
        self.eng = eng
        self.fn = fn
        self.deps = ()
        self.marked = False
        self.count = 0
        self.sig = sig
        self.dma = dma


class Prog:
    ENGS = ("pe", "act", "dve", "pool", "sp")

    def __init__(self):
        self.ops = []
        self.eng_ops = {e: [] for e in self.ENGS}
        self.lastw = {}
        self.readers = {}
        self.dma_last = {}
        self.dma_n = {}

    def add(self, eng, fn, reads=(), writes=(), dma_key=None):
        oid = len(self.ops)
        sig = eng if dma_key is None else ("dma", dma_key)
        op = Op(eng, fn, sig, dma_key is not None)
        deps = {}
        lastw = self.lastw
        readers = self.readers
        ops = self.ops

        def need(d):
            s = ops[d].sig
            if s == eng and (eng == "pe" or not SAME_ENGINE_SYNC):
                return
            if deps.get(s, -1) < d:
                deps[s] = d

        for r in reads:
            w = lastw.get(r)
            if w is not None:
                need(w)
        for r in writes:
            w = lastw.get(r)
            if w is not None:
                need(w)
            rd = readers.get(r)
            if rd:
                for d in rd.values():
                    need(d)
        if dma_key is not None:
            p = self.dma_last.get(dma_key)
            if p is not None:
                need(p)
            self.dma_last[dma_key] = oid
            n = self.dma_n.get(dma_key, 0) + 1
            self.dma_n[dma_key] = n
            op.count = 16 * n
            op.marked = True
        for r in writes:
            lastw[r] = oid
            readers[r] = {}
        for r in reads:
            rd = readers.get(r)
            if rd is None:
                rd = readers[r] = {}
            rd[sig] = oid
        op.deps = tuple(deps.values())
        for d in op.deps:
            ops[d].marked = True
        ops.append(op)
        self.eng_ops[eng].append(oid)
        return oid

    def emit(self, nc, stack):
        ops = self.ops
        sems = {}
        for e in self.ENGS:
            sems[e] = stack.enter_context(nc.semaphore("s_" + e))
            c = 0
            for oid in self.eng_ops[e]:
                op = ops[oid]
                if not op.dma and op.marked:
                    c += 1
                    op.count = c
            assert c < 60000, (e, c)
        for i, k in enumerate(self.dma_n):
            assert 16 * self.dma_n[k] < 60000, (k, self.dma_n[k])
            sems[("dma", k)] = stack.enter_context(nc.semaphore("d%d" % i))
        block = stack.enter_context(nc.Block())

        def run(e_name, eng):
            waited = {}
            for oid in self.eng_ops[e_name]:
                op = ops[oid]
                for d in op.deps:
                    dop = ops[d]
                    if waited.get(dop.sig, 0) < dop.count:
                        eng.wait_ge(sems[dop.sig], dop.count)
                        waited[dop.sig] = dop.count
                if op.fn is None:
                    continue
                ins = op.fn(eng)
                if op.dma:
                    ins.then_inc(sems[op.sig], 16)
                elif op.marked:
                    ins.then_inc(sems[op.sig], 1)

        @block.tensor
        def _(eng):
            run("pe", eng)

        @block.scalar
        def _(eng):
            run("act", eng)

        @block.vector
        def _(eng):
            run("dve", eng)

        @block.gpsimd
        def _(eng):
            run("pool", eng)

        @block.sync
        def _(eng):
            run("sp", eng)


class Region:
    def __init__(self, nc, stack, name, nbytes):
        self.name = name
        self.nbytes = nbytes
        self.t = stack.enter_context(nc.sbuf_tensor(name, [128, nbytes // 4], F32))


class LT:
    def __init__(self, reg, off, dt, rows, cols):
        self.reg = reg
        self.off = off
        self.esz = 2 if dt == BF16 else 4
        self.rows = rows
        self.cols = cols
        nb = rows * cols * self.esz
        assert off % 4 == 0 and nb % 4 == 0 and off + nb <= reg.nbytes, (reg.name, off, nb)
        a = reg.t[:, off // 4:(off + nb) // 4]
        if dt != F32:
            a = a.bitcast(dt)
        self.ap3 = a.rearrange("p (r c) -> p r c", c=cols)

    def ap(self, r, c0, c1, p0=0, p1=128):
        return self.ap3[p0:p1, r, c0:c1]

    def aps(self, r0, r1, c0, c1, p0=0, p1=128):
        return self.ap3[p0:p1, r0:r1, c0:c1]

    def gr(self, r0, r1, c0, c1):
        out = []
        nm = self.reg.name
        for r in range(r0, r1):
            b0 = self.off + (r * self.cols + c0) * self.esz
            b1 = self.off + (r * self.cols + c1) * self.esz
            for g in range(b0 // 256, (b1 + 255) // 256):
                out.append((nm, g))
        return out

    def g1(self, r, c0, c1):
        return self.gr(r, r + 1, c0, c1)


def layer_tiles(out_start, n_end):
    n = n_end - out_start
    tiles = []
    rem = n % 512
    g = out_start
    if rem:
        tiles.append((g, rem))
        g += rem
    while g < n_end:
        tiles.append((g, 512))
        g += 512
    return tiles


def build_program(NL):
    N0 = TOK_CORE + HALO * NL
    S0 = HALO * NL
    nc = bass.Bass("TRN2", target_bir_lowering=False)
    P = Prog()
    stack = contextlib.ExitStack()

    def din(name, shape, dt=F32):
        return nc.dram_tensor(name, list(shape), dt, kind="ExternalInput").ap()

    xin0 = din("xin", [D, N0])
    posr = din("posr", [128, N0], I32)
    win = din("win", [NL * N_WIN, 128, 4096])
    wbr = din("wbr", [NL * N_WBR, 128, 3072])
    wo = din("wo", [NL * N_WO, 128, 4096])
    wup = din("wup", [NL * N_WUP, 128, 4096])
    wdn = din("wdn", [NL * N_WDN, 128, 4096])
    lnp = din("lnp", [128, NL * 4 * 32])
    cwp = din("cwp", [128, NL * 24])
    snk = din("snk", [128, NL * 32])
    cst = din("cst", [128, 3 * 512])
    perm_d = din("perm", [128, 128])
    vec = din("vec", [128, 4])
    yout = nc.dram_tensor("yout", [D, TOK_CORE], F32, kind="ExternalOutput").ap()
    ropeC_d = nc.dram_tensor("ropeC", [128, N0], F32, kind="Internal").ap()
    ropeS_d = nc.dram_tensor("ropeS", [128, N0], F32, kind="Internal").ap()
    xbufs = [nc.dram_tensor("xs%d" % i, [D, N0], F32, kind="Internal").ap() for i in range(2)] if NL > 1 else []

    RA = Region(nc, stack, "RA", 65536)
    RB = Region(nc, stack, "RB", 40960)
    RC = Region(nc, stack, "RC", 32768)
    RW = Region(nc, stack, "RW", NW * 8192)
    RM = Region(nc, stack, "RM", 24576)
    psb = [stack.enter_context(nc.psum_tensor("ps%d" % i, [128, 512], F32)) for i in range(8)]

    y = LT(RA, 0, F32, 32, 512)
    qT = LT(RA, 0, BF16, 16, 512)
    aT = LT(RA, 16384, BF16, 16, 512)
    yconv = LT(RA, 32768, BF16, 8, 512)
    kT = LT(RA, 40960, BF16, 4, 640)
    Vt = LT(RA, 46080, BF16, 5, 512)
    PT = LT(RA, 51200, BF16, 4, 512)
    rC = LT(RA, 55296, F32, 1, 640)
    rS = LT(RA, 57856, F32, 1, 640)
    tmp2 = LT(RA, 0, F32, 8, 512)
    setupT = LT(RA, 0, F32, 6, N0)
    xT = LT(RB, 0, BF16, 32, 640)
    hT = LT(RB, 0, BF16, 32, 512)
    tmp4f = LT(RB, 32768, F32, 2, 512)
    tmp4b = LT(RB, 36864, BF16, 4, 512)
    tmp3f = LT(RB, 0, F32, 8, 512)
    tmp3b = LT(RB, 16384, BF16, 4, 512)
    tmpL = LT(RB, 0, F32, 4, 512)
    tmp1 = LT(RC, 0, F32, 12, 640)
    mg = LT(RC, 0, BF16, 32, 512)
    x1T = LT(RC, 0, BF16, 32, 512)
    wr = LT(RW, 0, BF16, NW, 4096)
    ostage = LT(RM, 0, F32, 2, 512)
    lnt = LT(RM, 4096, F32, 3, 512)
    masks = LT(RM, 10240, BF16, 3, 512)
    ones = LT(RM, 13312, BF16, 1, 128)
    perm = LT(RM, 13568, F32, 1, 128)
    lnps = LT(RM, 14080, F32, 1, NL * 128)
    lnpa = LT(RM, 16128, F32, 1, NL * 128)
    cws = LT(RM, 18176, F32, 1, NL * 24)
    esk = LT(RM, 18560, F32, 1, NL * 32)
    vecs = LT(RM, 19072, F32, 1, 4)

    def PS(b):
        return ("ps", b)

    bank_rr = [0]

    def next_bank():
        b = bank_rr[0]
        bank_rr[0] = (b + 1) % 6
        return b

    slot_rr = [0]

    def load_slab(src_ap, kc):
        s = slot_rr[0]
        slot_rr[0] = (s + 1) % NW
        dst = wr.ap(s, 0, kc * 128)
        P.add("pool", lambda e, d=dst, a=src_ap: e.dma_start(out=d, in_=a), writes=[("w", s)], dma_key=("w", s))
        return s

    def wsl(s, kc):
        return wr.ap(s, kc * 128, (kc + 1) * 128)

    def mm(out_ap, lhsT, rhs, start, stop, reads, bank):
        P.add("pe", lambda e: e.matmul(out_ap, lhsT, rhs, start=start, stop=stop), reads=reads, writes=[PS(bank)])

    def act(out_ap, in_ap, func, reads, writes, bias=None, scale=None):
        kw = {}
        if bias is not None:
            kw["bias"] = bias
        if scale is not None:
            kw["scale"] = scale
        P.add("act", lambda e: e.activation(out_ap, in_ap, func, **kw), reads=reads, writes=writes)

    def tt(out_ap, in0, in1, op, reads, writes):
        P.add("dve", lambda e: e.tensor_tensor(out_ap, in0, in1, op), reads=reads, writes=writes)

    def ts(out_ap, in0, s1, s2, op0, op1, reads, writes):
        if s2 is None:
            P.add("dve", lambda e: e.tensor_scalar(out_ap, in0, s1, None, op0), reads=reads, writes=writes)
        else:
            P.add("dve", lambda e: e.tensor_scalar(out_ap, in0, s1, s2, op0, op1), reads=reads, writes=writes)

    def stt(out_ap, in0, sc, in1, op0, op1, reads, writes):
        P.add("dve", lambda e: e.scalar_tensor_tensor(out_ap, in0, sc, in1, op0, op1), reads=reads, writes=writes)

    def sp_load(dst_lt, r, c0, c1, src, key):
        P.add("sp", lambda e: e.dma_start(out=dst_lt.ap(r, c0, c1), in_=src), writes=dst_lt.g1(r, c0, c1), dma_key=key)

    sp_load(perm, 0, 0, 128, perm_d, "c0")
    sp_load(lnps, 0, 0, NL * 128, lnp, "c1")
    sp_load(cws, 0, 0, NL * 24, cwp, "c2")
    sp_load(esk, 0, 0, NL * 32, snk, "c3")
    sp_load(vecs, 0, 0, 4, vec, "c4")
    for i in range(3):
        P.add("pool", lambda e, i=i: e.dma_start(out=masks.ap(i, 0, 512), in_=cst[:, i * 512:(i + 1) * 512]),
              writes=masks.g1(i, 0, 512), dma_key=("m", i))
    P.add("dve", lambda e: e.memset(ones.ap(0, 0, 128), 1.0), writes=ones.g1(0, 0, 128))
    act(esk.ap(0, 0, NL * 32), esk.ap(0, 0, NL * 32), AF.Exp, esk.g1(0, 0, NL * 32), esk.g1(0, 0, NL * 32))
    P.add("dve", lambda e: e.tensor_scalar(lnpa.ap(0, 0, NL * 128), lnps.ap(0, 0, NL * 128), ALPHA, None, ALU.mult),
          reads=lnps.g1(0, 0, NL * 128), writes=lnpa.g1(0, 0, NL * 128))
    posi = setupT.ap3[:, 0, :].bitcast(I32)
    P.add("sp", lambda e: e.dma_start(out=posi, in_=posr), writes=setupT.g1(0, 0, N0), dma_key="c5")
    P.add("dve", lambda e: e.tensor_copy(setupT.ap(1, 0, N0), posi), reads=setupT.g1(0, 0, N0), writes=setupT.g1(1, 0, N0))
    fq = vecs.ap(0, 0, 1)
    sg = vecs.ap(0, 1, 2)
    flag = vecs.ap(0, 2, 3)
    vg = vecs.g1(0, 0, 4)
    epsb = vecs.ap(0, 3, 4)
    C1 = 6.28125
    C2 = TWO_PI - 6.28125
    SH = 1.0 - 1e-6

    def sT(r):
        return setupT.ap(r, 0, N0)

    def sG(r):
        return setupT.g1(r, 0, N0)

    ki_ap = setupT.ap3[:, 0, :].bitcast(I32)
    ts(sT(2), sT(1), fq, None, ALU.mult, None, sG(1) + vg, sG(2))
    ts(sT(1), sT(2), 1.0 / TWO_PI, None, ALU.mult, None, sG(2), sG(1))
    P.add("dve", lambda e: e.tensor_copy(ki_ap, sT(1)), reads=sG(1), writes=sG(0))
    P.add("dve", lambda e: e.tensor_copy(sT(1), ki_ap), reads=sG(0), writes=sG(1))
    stt(sT(3), sT(1), -C1, sT(2), ALU.mult, ALU.add, sG(1) + sG(2), sG(3))
    stt(sT(3), sT(1), -C2, sT(3), ALU.mult, ALU.add, sG(1) + sG(3), sG(3))

    def fold(r, tmp):
        ts(sT(tmp), sT(r), -math.pi, 1e9, ALU.add, ALU.mult, sG(r), sG(tmp))
        ts(sT(tmp), sT(tmp), 0.0, 1.0, ALU.max, ALU.min, sG(tmp), sG(tmp))
        stt(sT(r), sT(tmp), -TWO_PI, sT(r), ALU.mult, ALU.add, sG(tmp) + sG(r), sG(r))

    fold(3, 1)
    act(sT(4), sT(3), AF.Sin, sG(3), sG(4), scale=SH)
    ts(sT(4), sT(4), sg, None, ALU.mult, None, sG(4) + vg, sG(4))
    ts(sT(2), sT(3), 0.5 * math.pi, None, ALU.add, None, sG(3), sG(2))
    fold(2, 1)
    act(sT(5), sT(2), AF.Sin, sG(2), sG(5), scale=SH)
    P.add("sp", lambda e: e.dma_start(out=ropeC_d, in_=setupT.ap(5, 0, N0)), reads=setupT.g1(5, 0, N0), writes=[("ropeC",)], dma_key="c6")
    P.add("sp", lambda e: e.dma_start(out=ropeS_d, in_=setupT.ap(4, 0, N0)), reads=setupT.g1(4, 0, N0), writes=[("ropeS",)], dma_key="c7")

    out_store_keys = []

    for l in range(NL):
        in_start = HALO * l
        out_start = HALO * (l + 1)
        if l == 0:
            xin = xin0
            xin_off = 0
        else:
            xin = xbufs[(l - 1) % 2]
            xin_off = 0
        last = (l == NL - 1)
        xo = yout if last else xbufs[l % 2]
        xo_off = -S0 if last else 0
        xin_v = xin.rearrange("(c p) t -> p c t", p=128)
        g1c = l * 128
        b1c = l * 128 + 32
        g2c = l * 128 + 64
        b2c = l * 128 + 96

        for (g0, T) in layer_tiles(out_start, N0):
            TH = T + 128
            gi0 = g0 - 128
            special = (g0 == S0)
            NB = T // 128

            for part in range(4):
                c0, c1 = part * 8, part * 8 + 8
                P.add("pool", lambda e, c0=c0, c1=c1, TH=TH, gi0=gi0, xin_v=xin_v:
                      e.dma_start(out=xT.aps(c0, c1, 0, TH), in_=xin_v[:, c0:c1, gi0:gi0 + TH]),
                      reads=[("xd", id(xin), c, (gi0 + k * 128) // 128) for c in range(c0, c1) for k in range(TH // 128)],
                      writes=xT.gr(c0, c1, 0, TH), dma_key=("xT", part))
            P.add("sp", lambda e, TH=TH, gi0=gi0: e.dma_start(out=rC.ap(0, 0, TH), in_=ropeC_d[:, gi0:gi0 + TH]),
                  reads=[("ropeC",)], writes=rC.g1(0, 0, TH), dma_key="rC")
            P.add("sp", lambda e, TH=TH, gi0=gi0: e.dma_start(out=rS.ap(0, 0, TH), in_=ropeS_d[:, gi0:gi0 + TH]),
                  reads=[("ropeS",)], writes=rS.g1(0, 0, TH), dma_key="rS")

            t1rr = [0]

            def t1slot():
                s = t1rr[0]
                t1rr[0] = (s + 1) % 12
                return s

            def rope(bank, c0t, n, dst_lt, dst_r, dst_c0):
                a = t1slot(); b_ = t1slot(); c_ = t1slot()
                pb = psb[bank]
                act(tmp1.ap(a, 0, n), pb[:, 0:n], AF.Copy, [PS(bank)], tmp1.g1(a, 0, n))
                bk2 = next_bank()
                P.add("pe", lambda e: e.matmul(psb[bk2][:, 0:n], perm.ap(0, 0, 128), tmp1.ap(a, 0, n), start=True, stop=True),
                      reads=perm.g1(0, 0, 128) + tmp1.g1(a, 0, n), writes=[PS(bk2)])
                tt(tmp1.ap(b_, 0, n), tmp1.ap(a, 0, n), rC.ap(0, c0t, c0t + n), ALU.mult,
                   tmp1.g1(a, 0, n) + rC.g1(0, c0t, c0t + n), tmp1.g1(b_, 0, n))
                tt(tmp1.ap(c_, 0, n), psb[bk2][:, 0:n], rS.ap(0, c0t, c0t + n), ALU.mult,
                   [PS(bk2)] + rS.g1(0, c0t, c0t + n), tmp1.g1(c_, 0, n))
                tt(dst_lt.ap(dst_r, dst_c0, dst_c0 + n), tmp1.ap(b_, 0, n), tmp1.ap(c_, 0, n), ALU.add,
                   tmp1.g1(b_, 0, n) + tmp1.g1(c_, 0, n), dst_lt.g1(dst_r, dst_c0, dst_c0 + n))

            wbase = l * N_WIN
            for r in range(16):
                s = load_slab(win[wbase + r], 32)
                bk = next_bank()
                for kc in range(32):
                    mm(psb[bk][:, 0:T], wsl(s, kc), xT.ap(kc, 128, TH), kc == 0, kc == 31,
                       [("w", s)] + xT.g1(kc, 128, TH), bk)
                rope(bk, 128, T, qT, r, 0)
            for r in range(4):
                s = load_slab(win[wbase + 16 + r], 32)
                bk = next_bank()
                for kc in range(32):
                    mm(psb[bk][:, 0:T], wsl(s, kc), xT.ap(kc, 128, TH), kc == 0, kc == 31,
                       [("w", s)] + xT.g1(kc, 128, TH), bk)
                bkh = next_bank()
                for kc in range(32):
                    mm(psb[bkh][:, 0:128], wsl(s, kc), xT.ap(kc, 0, 128), kc == 0, kc == 31,
                       [("w", s)] + xT.g1(kc, 0, 128), bkh)
                rope(bk, 128, T, kT, r, 128)
                rope(bkh, 0, 128, kT, r, 0)
            for c in range(8):
                cwb = l * 24 + c * 3
                w0 = cws.ap(0, cwb, cwb + 1); w1 = cws.ap(0, cwb + 1, cwb + 2); w2 = cws.ap(0, cwb + 2, cwb + 3)
                cg = cws.g1(0, cwb, cwb + 3)
                s = load_slab(win[wbase + 20 + c * 3 + 0], 32)
                bka = next_bank()
                for kc in range(32):
                    mm(psb[bka][:, 0:T], wsl(s, kc), xT.ap(kc, 128, TH), kc == 0, kc == 31, [("w", s)] + xT.g1(kc, 128, TH), bka)
                for kc in range(32):
                    mm(psb[6][:, 0:2], wsl(s, kc), xT.ap(kc, 126, 128), kc == 0, kc == 31, [("w", s)] + xT.g1(kc, 126, 128), 6)
                ta = t1slot(); tu = t1slot(); tb = t1slot(); tc_ = t1slot()
                act(tmp1.ap(ta, 2, 2 + T), psb[bka][:, 0:T], AF.Copy, [PS(bka)], tmp1.g1(ta, 2, 2 + T))
                if special:
                    act(tmp1.ap(ta, 0, 2), psb[6][:, 0:2], AF.Identity, [PS(6)] + vg, tmp1.g1(ta, 0, 2), scale=flag)
                else:
                    act(tmp1.ap(ta, 0, 2), psb[6][:, 0:2], AF.Copy, [PS(6)], tmp1.g1(ta, 0, 2))
                s = load_slab(win[wbase + 20 + c * 3 + 1], 32)
                bkb = next_bank()
                for kc in range(32):
                    mm(psb[bkb][:, 0:T], wsl(s, kc), xT.ap(kc, 128, TH), kc == 0, kc == 31, [("w", s)] + xT.g1(kc, 128, TH), bkb)
                for kc in range(32):
                    mm(psb[7][:, 0:2], wsl(s, kc), xT.ap(kc, 126, 128), kc == 0, kc == 31, [("w", s)] + xT.g1(kc, 126, 128), 7)
                tt(tmp1.ap(tu, 2, 2 + T), tmp1.ap(ta, 2, 2 + T), psb[bkb][:, 0:T], ALU.mult,
                   tmp1.g1(ta, 2, 2 + T) + [PS(bkb)], tmp1.g1(tu, 2, 2 + T))
                tt(tmp1.ap(tu, 0, 2), tmp1.ap(ta, 0, 2), psb[7][:, 0:2], ALU.mult,
                   tmp1.g1(ta, 0, 2) + [PS(7)], tmp1.g1(tu, 0, 2))
                ts(tmp1.ap(tb, 0, T), tmp1.ap(tu, 0, T), w0, None, ALU.mult, None, tmp1.g1(tu, 0, T) + cg, tmp1.g1(tb, 0, T))
                stt(tmp1.ap(tc_, 0, T), tmp1.ap(tu, 1, 1 + T), w1, tmp1.ap(tb, 0, T), ALU.mult, ALU.add,
                    tmp1.g1(tu, 1, 1 + T) + tmp1.g1(tb, 0, T) + cg, tmp1.g1(tc_, 0, T))
                stt(tmp1.ap(tb, 0, T), tmp1.ap(tu, 2, 2 + T), w2, tmp1.ap(tc_, 0, T), ALU.mult, ALU.add,
                    tmp1.g1(tu, 2, 2 + T) + tmp1.g1(tc_, 0, T) + cg, tmp1.g1(tb, 0, T))
                s = load_slab(win[wbase + 20 + c * 3 + 2], 32)
                bkc = next_bank()
                for kc in range(32):
                    mm(psb[bkc][:, 0:T], wsl(s, kc), xT.ap(kc, 128, TH), kc == 0, kc == 31, [("w", s)] + xT.g1(kc, 128, TH), bkc)
                tt(yconv.ap(c, 0, T), tmp1.ap(tb, 0, T), psb[bkc][:, 0:T], ALU.mult,
                   tmp1.g1(tb, 0, T) + [PS(bkc)], yconv.g1(c, 0, T))
            for j in range(4):
                s = load_slab(win[wbase + 44 + j], 32)
                for blk in range(NB + 1):
                    bk = next_bank()
                    po = psb[bk][:, 0:128]
                    for kc in range(32):
                        mm(po, xT.ap(kc, blk * 128, blk * 128 + 128), wsl(s, kc), kc == 0, kc == 31,
                           [("w", s)] + xT.g1(kc, blk * 128, blk * 128 + 128), bk)
                    act(Vt.ap(blk, j * 128, j * 128 + 128), po, AF.Copy, [PS(bk)], Vt.g1(blk, j * 128, j * 128 + 128))
            ptrr = 0
            for n in range(NB):
                for g in range(4):
                    for half in range(2):
                        p0, p1 = half * 64, half * 64 + 64
                        pts = []
                        for kb in range(2):
                            kc0 = (n + kb) * 128
                            bk = next_bank()
                            P.add("pe", lambda e, bk=bk, g=g, kc0=kc0, p0=p0, p1=p1, n=n:
                                  e.matmul(psb[bk][:, 0:512], kT.ap(g, kc0, kc0 + 128, p0, p1),
                                           qT.aps(4 * g, 4 * g + 4, n * 128, n * 128 + 128, p0, p1), start=True, stop=True),
                                  reads=kT.g1(g, kc0, kc0 + 128) + qT.gr(4 * g, 4 * g + 4, n * 128, n * 128 + 128),
                                  writes=[PS(bk)])
                            pi = ptrr
                            ptrr = (ptrr + 1) % 4
                            act(PT.ap(pi, 0, 512), psb[bk][:, 0:512], AF.Exp, [PS(bk)], PT.g1(pi, 0, 512), scale=0.125)
                            if kb == 1:
                                mi = 1
                            else:
                                mi = 2 if (special and n == 0) else 0
                            tt(PT.ap(pi, 0, 512), PT.ap(pi, 0, 512), masks.ap(mi, 0, 512), ALU.mult,
                               PT.g1(pi, 0, 512) + masks.g1(mi, 0, 512), PT.g1(pi, 0, 512))
                            pts.append(pi)
                        bo = next_bank()
                        for kb in range(2):
                            P.add("pe", lambda e, bo=bo, kb=kb, g=g, n=n, pi=pts[kb]:
                                  e.matmul(psb[bo][:, 0:512], Vt.ap(n + kb, g * 128, g * 128 + 128), PT.ap(pi, 0, 512),
                                           start=(kb == 0), stop=(kb == 1)),
                                  reads=Vt.g1(n + kb, g * 128, g * 128 + 128) + PT.g1(pts[kb], 0, 512), writes=[PS(bo)])
                        bd = next_bank()
                        for kb in range(2):
                            P.add("pe", lambda e, bd=bd, kb=kb, pi=pts[kb]:
                                  e.matmul(psb[bd][:, 0:512], ones.ap(0, 0, 128), PT.ap(pi, 0, 512),
                                           start=(kb == 0), stop=(kb == 1)),
                                  reads=ones.g1(0, 0, 128) + PT.g1(pts[kb], 0, 512), writes=[PS(bd)])
                        rt = t1slot()
                        for h in range(4):
                            hd = l * 32 + (8 * g + 4 * half + h)
                            P.add("dve", lambda e, rt=rt, h=h, bd=bd, hd=hd, p0=p0, p1=p1:
                                  e.tensor_scalar(tmp1.ap(rt, h * 128, h * 128 + 128, p0, p1), psb[bd][p0:p1, h * 128:h * 128 + 128],
                                                  esk.ap(0, hd, hd + 1, p0, p1), None, ALU.add),
                                  reads=[PS(bd)] + esk.g1(0, hd, hd + 1), writes=tmp1.g1(rt, h * 128, h * 128 + 128))
                        P.add("dve", lambda e, rt=rt, p0=p0, p1=p1: e.reciprocal(tmp1.ap(rt, 0, 512, p0, p1), tmp1.ap(rt, 0, 512, p0, p1)),
                              reads=tmp1.g1(rt, 0, 512), writes=tmp1.g1(rt, 0, 512))
                        P.add("dve", lambda e, rt=rt, bo=bo, g=g, n=n, p0=p0, p1=p1:
                              e.tensor_tensor(aT.aps(4 * g, 4 * g + 4, n * 128, n * 128 + 128, p0, p1),
                                              psb[bo][p0:p1, 0:512].rearrange("p (h q) -> p h q", q=128),
                                              tmp1.ap3[p0:p1, rt, 0:512].rearrange("p (h q) -> p h q", q=128), ALU.mult),
                              reads=[PS(bo)] + tmp1.g1(rt, 0, 512),
                              writes=aT.gr(4 * g, 4 * g + 4, n * 128, n * 128 + 128))
            t2rr = 0
            for m in range(32):
                s = load_slab(win[wbase + 48 + 2 * m], 32)
                bg = next_bank()
                for kc in range(32):
                    mm(psb[bg][:, 0:T], wsl(s, kc), xT.ap(kc, 128, TH), kc == 0, kc == 31, [("w", s)] + xT.g1(kc, 128, TH), bg)
                ia = t2rr; ib = t2rr + 1; ic = t2rr + 2; idd = t2rr + 3
                t2rr = (t2rr + 4) % 8
                act(tmp2.ap(ia, 0, T), psb[bg][:, 0:T], AF.Sigmoid, [PS(bg)], tmp2.g1(ia, 0, T))
                s = load_slab(win[wbase + 48 + 2 * m + 1], 32)
                bg2 = next_bank()
                for kc in range(32):
                    mm(psb[bg2][:, 0:T], wsl(s, kc), xT.ap(kc, 128, TH), kc == 0, kc == 31, [("w", s)] + xT.g1(kc, 128, TH), bg2)
                act(tmp2.ap(ib, 0, T), psb[bg2][:, 0:T], AF.Sigmoid, [PS(bg2)], tmp2.g1(ib, 0, T))
                s = load_slab(wbr[l * N_WBR + m], 24)
                ba = next_bank()
                for kc in range(16):
                    mm(psb[ba][:, 0:T], wsl(s, kc), aT.ap(kc, 0, T), kc == 0, kc == 15, [("w", s)] + aT.g1(kc, 0, T), ba)
                bc = next_bank()
                for kc in range(8):
                    mm(psb[bc][:, 0:T], wsl(s, 16 + kc), yconv.ap(kc, 0, T), kc == 0, kc == 7, [("w", s)] + yconv.g1(kc, 0, T), bc)
                tt(tmp2.ap(ic, 0, T), tmp2.ap(ia, 0, T), psb[ba][:, 0:T], ALU.mult, tmp2.g1(ia, 0, T) + [PS(ba)], tmp2.g1(ic, 0, T))
                tt(tmp2.ap(idd, 0, T), tmp2.ap(ib, 0, T), psb[bc][:, 0:T], ALU.mult, tmp2.g1(ib, 0, T) + [PS(bc)], tmp2.g1(idd, 0, T))
                tt(mg.ap(m, 0, T), tmp2.ap(ic, 0, T), tmp2.ap(idd, 0, T), ALU.add,
                   tmp2.g1(ic, 0, T) + tmp2.g1(idd, 0, T), mg.g1(m, 0, T))

            def stats(m, ybf_lt, ybf_r, ysq_r):
                act(ybf_lt.ap(ybf_r, 0, T), y.ap(m, 0, T), AF.Copy, y.g1(m, 0, T), ybf_lt.g1(ybf_r, 0, T))
                act(ybf_lt.ap(ysq_r, 0, T), y.ap(m, 0, T), AF.Square, y.g1(m, 0, T), ybf_lt.g1(ysq_r, 0, T))
                mm(psb[6][:, 0:T], ones.ap(0, 0, 128), ybf_lt.ap(ybf_r, 0, T), m == 0, m == 31,
                   ones.g1(0, 0, 128) + ybf_lt.g1(ybf_r, 0, T), 6)
                mm(psb[7][:, 0:T], ones.ap(0, 0, 128), ybf_lt.ap(ysq_r, 0, T), m == 0, m == 31,
                   ones.g1(0, 0, 128) + ybf_lt.g1(ysq_r, 0, T), 7)

            def ln_finish():
                ts(lnt.ap(2, 0, T), psb[6][:, 0:T], 1.0 / D, None, ALU.mult, None, [PS(6)], lnt.g1(2, 0, T))
                tt(lnt.ap(1, 0, T), lnt.ap(2, 0, T), lnt.ap(2, 0, T), ALU.mult, lnt.g1(2, 0, T), lnt.g1(1, 0, T))
                stt(lnt.ap(0, 0, T), psb[7][:, 0:T], 1.0 / D, lnt.ap(1, 0, T), ALU.mult, ALU.subtract,
                    [PS(7)] + lnt.g1(1, 0, T), lnt.g1(0, 0, T))
                act(lnt.ap(0, 0, T), lnt.ap(0, 0, T), AF.Sqrt, lnt.g1(0, 0, T) + vg, lnt.g1(0, 0, T), bias=epsb, scale=1.0)
                P.add("dve", lambda e: e.reciprocal(lnt.ap(0, 0, T), lnt.ap(0, 0, T)), reads=lnt.g1(0, 0, T), writes=lnt.g1(0, 0, T))
                stt(lnt.ap(1, 0, T), lnt.ap(2, 0, T), -1.0, lnt.ap(0, 0, T), ALU.mult, ALU.mult,
                    lnt.g1(2, 0, T) + lnt.g1(0, 0, T), lnt.g1(1, 0, T))

            for m in range(32):
                s = load_slab(wo[l * N_WO + m], 32)
                bk = next_bank()
                for kc in range(32):
                    mm(psb[bk][:, 0:T], wsl(s, kc), mg.ap(kc, 0, T), kc == 0, kc == 31, [("w", s)] + mg.g1(kc, 0, T), bk)
                xs = m % 4
                P.add("sp", lambda e, xs=xs, m=m, g0=g0, T=T, xin=xin:
                      e.dma_start(out=tmp3f.ap(xs, 0, T), in_=xin[m * 128:(m + 1) * 128, g0:g0 + T]),
                      reads=[("xd", id(xin), m, (g0 + k * 128) // 128) for k in range(T // 128)],
                      writes=tmp3f.g1(xs, 0, T), dma_key=("xr", xs))
                stt(y.ap(m, 0, T), tmp3f.ap(xs, 0, T), ALPHA, psb[bk][:, 0:T], ALU.mult, ALU.add,
                    tmp3f.g1(xs, 0, T) + [PS(bk)], y.g1(m, 0, T))
                stats(m, tmp3b, (m % 2) * 2, (m % 2) * 2 + 1)
            ln_finish()
            for m in range(32):
                ta = 4 + (m % 2) * 2
                tb = ta + 1
                tt(tmp3f.ap(ta, 0, T), y.ap(m, 0, T), lnt.ap(0, 0, T), ALU.mult, y.g1(m, 0, T) + lnt.g1(0, 0, T), tmp3f.g1(ta, 0, T))
                tt(tmp3f.ap(tb, 0, T), tmp3f.ap(ta, 0, T), lnt.ap(1, 0, T), ALU.add, tmp3f.g1(ta, 0, T) + lnt.g1(1, 0, T), tmp3f.g1(tb, 0, T))
                act(x1T.ap(m, 0, T), tmp3f.ap(tb, 0, T), AF.Identity, tmp3f.g1(tb, 0, T) + lnps.g1(0, g1c + m, g1c + m + 1) + lnps.g1(0, b1c + m, b1c + m + 1),
                    x1T.g1(m, 0, T), bias=lnps.ap(0, b1c + m, b1c + m + 1), scale=lnps.ap(0, g1c + m, g1c + m + 1))
                act(y.ap(m, 0, T), tmp3f.ap(tb, 0, T), AF.Identity, tmp3f.g1(tb, 0, T) + lnpa.g1(0, g1c + m, g1c + m + 1) + lnpa.g1(0, b1c + m, b1c + m + 1),
                    y.g1(m, 0, T), bias=lnpa.ap(0, b1c + m, b1c + m + 1), scale=lnpa.ap(0, g1c + m, g1c + m + 1))

            for gq in range(4):
                for j in range(32):
                    s = load_slab(wup[l * N_WUP + gq * 32 + j], 32)
                    bk = next_bank()
                    for kc in range(32):
                        mm(psb[bk][:, 0:T], wsl(s, kc), x1T.ap(kc, 0, T), kc == 0, kc == 31, [("w", s)] + x1T.g1(kc, 0, T), bk)
                    tr = j % 2
                    act(tmp4f.ap(tr, 0, T), psb[bk][:, 0:T], AF.Relu, [PS(bk)], tmp4f.g1(tr, 0, T))
                    tt(hT.ap(j, 0, T), tmp4f.ap(tr, 0, T), tmp4f.ap(tr, 0, T), ALU.mult, tmp4f.g1(tr, 0, T), hT.g1(j, 0, T))
                for m in range(32):
                    s = load_slab(wdn[l * N_WDN + gq * 32 + m], 32)
                    bk = next_bank()
                    for kc in range(32):
                        mm(psb[bk][:, 0:T], wsl(s, kc), hT.ap(kc, 0, T), kc == 0, kc == 31, [("w", s)] + hT.g1(kc, 0, T), bk)
                    tt(y.ap(m, 0, T), y.ap(m, 0, T), psb[bk][:, 0:T], ALU.add, y.g1(m, 0, T) + [PS(bk)], y.g1(m, 0, T))
                    if gq == 3:
                        stats(m, tmp4b, (m % 2) * 2, (m % 2) * 2 + 1)
            ln_finish()
            for m in range(32):
                ta = (m % 2) * 2
                tb = ta + 1
                os_ = m % 2
                tt(tmpL.ap(ta, 0, T), y.ap(m, 0, T), lnt.ap(0, 0, T), ALU.mult, y.g1(m, 0, T) + lnt.g1(0, 0, T), tmpL.g1(ta, 0, T))
                tt(tmpL.ap(tb, 0, T), tmpL.ap(ta, 0, T), lnt.ap(1, 0, T), ALU.add, tmpL.g1(ta, 0, T) + lnt.g1(1, 0, T), tmpL.g1(tb, 0, T))
                act(ostage.ap(os_, 0, T), tmpL.ap(tb, 0, T), AF.Identity,
                    tmpL.g1(tb, 0, T) + lnps.g1(0, g2c + m, g2c + m + 1) + lnps.g1(0, b2c + m, b2c + m + 1),
                    ostage.g1(os_, 0, T), bias=lnps.ap(0, b2c + m, b2c + m + 1), scale=lnps.ap(0, g2c + m, g2c + m + 1))
                lo = g0 + xo_off
                c_lo = 0
                if lo < 0:
                    c_lo = -lo
                if c_lo >= T:
                    continue
                wres = [("xd", id(xo), m, (g0 + k * 128) // 128) for k in range(T // 128)]
                P.add("sp", lambda e, os_=os_, m=m, lo=lo, c_lo=c_lo, T=T, xo=xo:
                      e.dma_start(out=xo[m * 128:(m + 1) * 128, lo + c_lo:lo + T], in_=ostage.ap(os_, c_lo, T)),
                      reads=ostage.g1(os_, 0, T), writes=wres, dma_key=("os", os_))
                if last:
                    out_store_keys.extend(wres)

    P.add("sp", None, reads=list(dict.fromkeys(out_store_keys)))
    P.emit(nc, stack)
    stack.close()
    return nc


Q0, K0, V0, CB0, CC0, CH0, GA0, GC0 = 0, 2048, 2304, 2560, 3584, 4608, 5632, 9728


def _head_order():
    rows = []
    for r in range(16):
        g, i = r // 4, r % 4
        rows.append((8 * g + i, 8 * g + 4 + i))
    return rows


def _win_cols():
    cols = []
    for (ha, hb) in _head_order():
        cols.append(np.concatenate([Q0 + ha * 64 + np.arange(64), Q0 + hb * 64 + np.arange(64)]))
    for g in range(4):
        k = K0 + g * 64 + np.arange(64)
        cols.append(np.concatenate([k, k]))
    for c in range(8):
        cols.append(CC0 + c * 128 + np.arange(128))
        cols.append(CH0 + c * 128 + np.arange(128))
        cols.append(CB0 + c * 128 + np.arange(128))
    for j in range(4):
        v = V0 + j * 64 + np.arange(64)
        cols.append(np.concatenate([v, v]))
    for m in range(32):
        cols.append(GA0 + m * 128 + np.arange(128))
        cols.append(GC0 + m * 128 + np.arange(128))
    return np.concatenate(cols)


def _slabs(W, cols=None, rows=None):
    if rows is not None:
        W = W[rows]
    if cols is not None:
        W = W[:, cols]
    K, N = W.shape
    return np.ascontiguousarray(W.reshape(K // 128, 128, N // 128, 128).transpose(2, 1, 0, 3)).reshape(N // 128, 128, K)


def _prep_layer(w_in, w_br_attn, w_br_conv, w_o, w_up, w_down):
    win = _slabs(w_in, cols=_win_cols())
    arow = np.concatenate([np.concatenate([ha * 64 + np.arange(64), hb * 64 + np.arange(64)]) for (ha, hb) in _head_order()])
    wbr = np.concatenate([_slabs(w_br_attn, rows=arow), _slabs(w_br_conv)], axis=2)
    wo = _slabs(w_o)
    wup = _slabs(w_up)
    wd = w_down.reshape(4, 32, 128, 32, 128).transpose(0, 3, 2, 1, 4)
    wdn = np.ascontiguousarray(wd).reshape(128, 128, 4096)
    return win, wbr, wo, wup, wdn


def _consts():
    j = np.arange(128)[:, None]
    i = np.arange(128)[None, :]
    mp = (j > i).astype(np.float32)
    mc = (j <= i).astype(np.float32)
    perm = np.zeros((128, 128), np.float32)
    fq = np.zeros(128, np.float32)
    sg = np.zeros(128, np.float32)
    inv_freq = (ROPE_THETA ** (-np.arange(0, 16, 2, dtype=np.float32) / np.float32(16))).astype(np.float32)
    for m in range(128):
        d = m % 64
        if d < 8:
            perm[m + 8, m] = 1.0
            fq[m] = inv_freq[d]
            sg[m] = -1.0
        elif d < 16:
            perm[m - 8, m] = 1.0
            fq[m] = inv_freq[d - 8]
            sg[m] = 1.0
    return np.tile(mp, (1, 4)), np.tile(mc, (1, 4)), perm, fq, sg


_PROG_CACHE = {}


def _get_prog(NL):
    if NL not in _PROG_CACHE:
        _PROG_CACHE[NL] = build_program(NL)
    return _PROG_CACHE[NL]


def _small_params(layers, conv_w, attn_sinks, ln1_g, ln1_b, ln2_g, ln2_b):
    hord = np.array([h for pair in _head_order() for h in pair])
    lnp = np.concatenate([np.concatenate([a[l].reshape(32, 128).T for a in (ln1_g, ln1_b, ln2_g, ln2_b)], axis=1) for l in layers], axis=1)
    cwp = np.concatenate([conv_w[l].reshape(3, 8, 128).transpose(2, 1, 0).reshape(128, 24) for l in layers], axis=1)
    snk = np.concatenate([np.tile(attn_sinks[l][None, :], (128, 1)) for l in layers], axis=1)
    return (np.ascontiguousarray(lnp, dtype=np.float32), np.ascontiguousarray(cwp, dtype=np.float32),
            np.ascontiguousarray(snk, dtype=np.float32))


def kernel(x, positions, w_in, conv_w, attn_sinks, w_br_attn, w_br_conv, w_o,
           ln1_g, ln1_b, w_up, w_down, ln2_g, ln2_b):
    x = np.asarray(x); positions = np.asarray(positions)
    B, S, _ = x.shape
    cps = NCORES // B
    mp4, mc4, perm, fq, sg = _consts()
    zeros4 = np.zeros_like(mp4)

    def run_layers(xT_full, layers):
        NL = len(layers)
        N0 = TOK_CORE + HALO * NL
        nc = _get_prog(NL)
        preps = [_prep_layer(np.asarray(w_in[l]), np.asarray(w_br_attn[l]), np.asarray(w_br_conv[l]),
                             np.asarray(w_o[l]), np.asarray(w_up[l]), np.asarray(w_down[l])) for l in layers]
        win = np.concatenate([p[0] for p in preps], axis=0)
        wbr = np.concatenate([p[1] for p in preps], axis=0)
        wo = np.concatenate([p[2] for p in preps], axis=0)
        wup = np.concatenate([p[3] for p in preps], axis=0)
        wdn = np.concatenate([p[4] for p in preps], axis=0)
        del preps
        lnp, cwp, snk = _small_params(layers, np.asarray(conv_w), np.asarray(attn_sinks), np.asarray(ln1_g),
                                      np.asarray(ln1_b), np.asarray(ln2_g), np.asarray(ln2_b))
        in_maps = []
        for c in range(NCORES):
            b, q = c // cps, c % cps
            s0 = q * TOK_CORE
            h = HALO * NL
            xin = np.zeros((D, N0), np.float32)
            pos = np.zeros((N0,), np.int32)
            lo = max(0, s0 - h)
            xin[:, h - (s0 - lo):] = xT_full[b][:, lo:s0 + TOK_CORE]
            pos[h - (s0 - lo):] = positions[b, lo:s0 + TOK_CORE]
            first = (q == 0)
            cst = np.concatenate([mp4, mc4, zeros4 if first else mp4], axis=1).astype(np.float32)
            vec = np.stack([fq, sg, np.full(128, 0.0 if first else 1.0, np.float32), np.full(128, LN_EPS, np.float32)], axis=1)
            in_maps.append({
                "xin": xin, "posr": np.ascontiguousarray(np.tile(pos[None, :], (128, 1))),
                "win": win, "wbr": wbr, "wo": wo, "wup": wup, "wdn": wdn,
                "lnp": lnp, "cwp": cwp, "snk": snk, "cst": cst, "perm": perm,
                "vec": np.ascontiguousarray(vec, dtype=np.float32),
            })
        res = run_bass_kernel_spmd(nc, in_maps, core_ids=list(range(NCORES)))
        out = np.empty_like(xT_full)
        for c in range(NCORES):
            b, q = c // cps, c % cps
            out[b][:, q * TOK_CORE:(q + 1) * TOK_CORE] = res.results[c]["yout"]
        return out

    xT_full = np.ascontiguousarray(np.transpose(x, (0, 2, 1)))
    if FUSED:
        xT_full = run_layers(xT_full, list(range(DEPTH)))
    else:
        for l in range(DEPTH):
            xT_full = run_layers(xT_full, [l])
    return np.ascontiguousarray(np.transpose(xT_full, (0, 2, 1))).astype(np.float32)
```

```python
import contextlib
import math
import numpy as np
import concourse.bass as bass
import concourse.mybir as mybir
from concourse.bass_utils import run_bass_kernel_spmd

F32 = mybir.dt.float32
BF16 = mybir.dt.bfloat16
I32 = mybir.dt.int32
AF = mybir.ActivationFunctionType
ALU = mybir.AluOpType

D = 4096
NCH = 32
DEPTH = 4
NCORES = 8
TOK_CORE = 2048
HALO = 128
ALPHA = (2 * DEPTH) ** 0.25
LN_EPS = 1e-5
ROPE_THETA = 500000.0
TWO_PI = 2.0 * math.pi
NW = 5
SAME_ENGINE_SYNC = True
FUSED = True

N_WIN = 112
N_WBR = 32
N_WO = 32
N_WUP = 128
N_WDN = 128


class Op:
    __slots__ = ("eng", "fn", "deps", "marked", "count", "sig", "dma")

    def __init__(self, eng, fn, sig, dma):
        self.eng = eng
        self.fn = fn
        self.deps = ()
        self.marked = False
        self.count = 0
        self.sig = sig
        self.dma = dma


class Prog:
    ENGS = ("pe", "act", "dve", "pool", "sp")

    def __init__(self):
        self.ops = []
        self.eng_ops = {e: [] for e in self.ENGS}
        self.lastw = {}
        self.readers = {}
        self.dma_last = {}
        self.dma_n = {}

    def add(self, eng, fn, reads=(), writes=(), dma_key=None):
        oid = len(self.ops)
        sig = eng if dma_key is None else ("dma", dma_key)
        op = Op(eng, fn, sig, dma_key is not None)
        deps = {}
        lastw = self.lastw
        readers = self.readers
        ops = self.ops

        def need(d):
            s = ops[d].sig
            if s == eng and (eng == "pe" or not SAME_ENGINE_SYNC):
                return
            if deps.get(s, -1) < d:
                deps[s] = d

        for r in reads:
            w = lastw.get(r)
            if w is not None:
                need(w)
        for r in writes:
            w = lastw.get(r)
            if w is not None:
                need(w)
            rd = readers.get(r)
            if rd:
                for d in rd.values():
                    need(d)
        if dma_key is not None:
            p = self.dma_last.get(dma_key)
            if p is not None:
                need(p)
            self.dma_last[dma_key] = oid
            n = self.dma_n.get(dma_key, 0) + 1
            self.dma_n[dma_key] = n
            op.count = 16 * n
            op.marked = True
        for r in writes:
            lastw[r] = oid
            readers[r] = {}
        for r in reads:
            rd = readers.get(r)
            if rd is None:
                rd = readers[r] = {}
            rd[sig] = oid
        op.deps = tuple(deps.values())
        for d in op.deps:
            ops[d].marked = True
        ops.append(op)
        self.eng_ops[eng].append(oid)
        return oid

    def emit(self, nc, stack):
        ops = self.ops
        sems = {}
        for e in self.ENGS:
            sems[e] = stack.enter_context(nc.semaphore("s_" + e))
            c = 0
            for oid in self.eng_ops[e]:
                op = ops[oid]
                if not op.dma and op.marked:
                    c += 1
                    op.count = c
            assert c < 60000, (e, c)
        for i, k in enumerate(self.dma_n):
            assert 16 * self.dma_n[k] < 60000, (k, self.dma_n[k])
            sems[("dma", k)] = stack.enter_context(nc.semaphore("d%d" % i))
        block = stack.enter_context(nc.Block())

        def run(e_name, eng):
            waited = {}
            for oid in self.eng_ops[e_name]:
                op = ops[oid]
                for d in op.deps:
                    dop = ops[d]
                    if waited.get(dop.sig, 0) < dop.count:
                        eng.wait_ge(sems[dop.sig], dop.count)
                        waited[dop.sig] = dop.count
                if op.fn is None:
                    continue
                ins = op.fn(eng)
                if op.dma:
                    ins.then_inc(sems[op.sig], 16)
                elif op.marked:
                    ins.then_inc(sems[op.sig], 1)

        @block.tensor
        def _(eng):
            run("pe", eng)

        @block.scalar
        def _(eng):
            run("act", eng)

        @block.vector
        def _(eng):
            run("dve", eng)

        @block.gpsimd
        def _(eng):
            run("pool", eng)

        @block.sync
        def _(eng):
            run("sp", eng)


class Region:
    def __init__(self, nc, stack, name, nbytes):
        self.name = name
        self.nbytes = nbytes
        self.t = stack.enter_context(nc.sbuf_tensor(name, [128, nbytes // 4], F32))


class LT:
    def __init__(self, reg, off, dt, rows, cols):
        self.reg = reg
        self.off = off
        self.esz = 2 if dt == BF16 else 4
        self.rows = rows
        self.cols = cols
        nb = rows * cols * self.esz
        assert off % 4 == 0 and nb % 4 == 0 and off + nb <= reg.nbytes, (reg.name, off, nb)
        a = reg.t[:, off // 4:(off + nb) // 4]
        if dt != F32:
            a = a.bitcast(dt)
        self.ap3 = a.rearrange("p (r c) -> p r c", c=cols)

    def ap(self, r, c0, c1, p0=0, p1=128):
        return self.ap3[p0:p1, r, c0:c1]

    def aps(self, r0, r1, c0, c1, p0=0, p1=128):
        return self.ap3[p0:p1, r0:r1, c0:c1]

    def gr(self, r0, r1, c0, c1):
        out = []
        nm = self.reg.name
        for r in range(r0, r1):
            b0 = self.off + (r * self.cols + c0) * self.esz
            b1 = self.off + (r * self.cols + c1) * self.esz
            for g in range(b0 // 256, (b1 + 255) // 256):
                out.append((nm, g))
        return out

    def g1(self, r, c0, c1):
        return self.gr(r, r + 1, c0, c1)


def layer_tiles(out_start, n_end):
    nb = (n_end - out_start) // 128
    nt = (nb + 3) // 4
    base, extra = nb // nt, nb % nt
    sizes = [base] * (nt - extra) + [base + 1] * extra
    tiles = []
    g = out_start
    for b in sizes:
        tiles.append((g, b * 128))
        g += b * 128
    assert g == n_end
    return tiles


def build_program(NL):
    N0 = TOK_CORE + HALO * NL
    S0 = HALO * NL
    nc = bass.Bass("TRN2", target_bir_lowering=False)
    P = Prog()
    stack = contextlib.ExitStack()

    def din(name, shape, dt=F32):
        return nc.dram_tensor(name, list(shape), dt, kind="ExternalInput").ap()

    xin0 = din("xin", [D, N0])
    posr = din("posr", [128, N0], I32)
    win = din("win", [NL * N_WIN, 128, 4096])
    wbr = din("wbr", [NL * N_WBR, 128, 3072])
    wo = din("wo", [NL * N_WO, 128, 4096])
    wup = din("wup", [NL * N_WUP, 128, 4096])
    wdn = din("wdn", [NL * N_WDN, 128, 4096])
    lnp = din("lnp", [128, NL * 4 * 32])
    cwp = din("cwp", [128, NL * 24])
    snk = din("snk", [128, NL * 32])
    cst = din("cst", [128, 3 * 512])
    perm_d = din("perm", [128, 128])
    vec = din("vec", [128, 4])
    yout = nc.dram_tensor("yout", [D, TOK_CORE], F32, kind="ExternalOutput").ap()
    ropeC_d = nc.dram_tensor("ropeC", [128, N0], F32, kind="Internal").ap()
    ropeS_d = nc.dram_tensor("ropeS", [128, N0], F32, kind="Internal").ap()
    xbufs = [nc.dram_tensor("xs%d" % i, [D, N0], F32, kind="Internal").ap() for i in range(2)] if NL > 1 else []

    RA = Region(nc, stack, "RA", 65536)
    RB = Region(nc, stack, "RB", 40960)
    RC = Region(nc, stack, "RC", 40960)
    RW = Region(nc, stack, "RW", NW * 8192)
    RM = Region(nc, stack, "RM", 19200)
    psb = [stack.enter_context(nc.psum_tensor("ps%d" % i, [128, 512], F32)) for i in range(8)]

    y = LT(RA, 0, F32, 32, 512)
    qT = LT(RA, 0, BF16, 16, 512)
    aT = LT(RA, 16384, BF16, 16, 512)
    yconv = LT(RA, 32768, BF16, 8, 512)
    kT = LT(RA, 40960, BF16, 4, 640)
    Vt = LT(RA, 46080, BF16, 5, 512)
    PT = LT(RA, 51200, BF16, 4, 512)
    rC = LT(RA, 55296, F32, 1, 640)
    rS = LT(RA, 57856, F32, 1, 640)
    tmp2 = LT(RA, 0, F32, 8, 512)
    setupT = LT(RA, 0, F32, 6, N0)
    def mkviews(RX, RY):
        return dict(
            xT=LT(RX, 0, BF16, 32, 640), hT=LT(RX, 0, BF16, 32, 512),
            tmp4f=LT(RX, 32768, F32, 2, 512), tmp4b=LT(RX, 36864, BF16, 4, 512),
            tmp3f=LT(RX, 0, F32, 8, 512), tmp3b=LT(RX, 16384, BF16, 4, 512), tmpL=LT(RX, 0, F32, 4, 512),
            tmp1=LT(RY, 0, F32, 12, 640), mg=LT(RY, 0, BF16, 32, 512), x1T=LT(RY, 0, BF16, 32, 512))

    VIEWS = [mkviews(RB, RC), mkviews(RC, RB)]
    wr = LT(RW, 0, BF16, NW, 4096)
    ostage = LT(RM, 0, F32, 2, 512)
    lnt = LT(RM, 4096, F32, 3, 512)
    masks = LT(RM, 10240, BF16, 3, 512)
    ones = LT(RM, 13312, BF16, 1, 128)
    perm = LT(RM, 13568, F32, 1, 128)
    lnps = LT(RM, 14080, F32, 1, NL * 128)
    lnpa = LT(RM, 16128, F32, 1, NL * 128)
    cws = LT(RM, 18176, F32, 1, NL * 24)
    esk = LT(RM, 18560, F32, 1, NL * 32)
    vecs = LT(RM, 19072, F32, 1, 4)

    def PS(b):
        return ("ps", b)

    bank_rr = [0]

    def next_bank():
        b = bank_rr[0]
        bank_rr[0] = (b + 1) % 6
        return b

    slot_rr = [0]

    def load_slab(src_ap, kc):
        s = slot_rr[0]
        slot_rr[0] = (s + 1) % NW
        dst = wr.ap(s, 0, kc * 128)
        P.add("pool", lambda e, d=dst, a=src_ap: e.dma_start(out=d, in_=a), writes=[("w", s)], dma_key=("w", s))
        return s

    def wsl(s, kc):
        return wr.ap(s, kc * 128, (kc + 1) * 128)

    def mm(out_ap, lhsT, rhs, start, stop, reads, bank):
        P.add("pe", lambda e: e.matmul(out_ap, lhsT, rhs, start=start, stop=stop), reads=reads, writes=[PS(bank)])

    def act(out_ap, in_ap, func, reads, writes, bias=None, scale=None):
        kw = {}
        if bias is not None:
            kw["bias"] = bias
        if scale is not None:
            kw["scale"] = scale
        P.add("act", lambda e: e.activation(out_ap, in_ap, func, **kw), reads=reads, writes=writes)

    def tt(out_ap, in0, in1, op, reads, writes):
        P.add("dve", lambda e: e.tensor_tensor(out_ap, in0, in1, op), reads=reads, writes=writes)

    def ts(out_ap, in0, s1, s2, op0, op1, reads, writes):
        if s2 is None:
            P.add("dve", lambda e: e.tensor_scalar(out_ap, in0, s1, None, op0), reads=reads, writes=writes)
        else:
            P.add("dve", lambda e: e.tensor_scalar(out_ap, in0, s1, s2, op0, op1), reads=reads, writes=writes)

    def stt(out_ap, in0, sc, in1, op0, op1, reads, writes):
        P.add("dve", lambda e: e.scalar_tensor_tensor(out_ap, in0, sc, in1, op0, op1), reads=reads, writes=writes)

    def sp_load(dst_lt, r, c0, c1, src, key):
        P.add("sp", lambda e: e.dma_start(out=dst_lt.ap(r, c0, c1), in_=src), writes=dst_lt.g1(r, c0, c1), dma_key=key)

    sp_load(perm, 0, 0, 128, perm_d, "c0")
    sp_load(lnps, 0, 0, NL * 128, lnp, "c1")
    sp_load(cws, 0, 0, NL * 24, cwp, "c2")
    sp_load(esk, 0, 0, NL * 32, snk, "c3")
    sp_load(vecs, 0, 0, 4, vec, "c4")
    for i in range(3):
        P.add("pool", lambda e, i=i: e.dma_start(out=masks.ap(i, 0, 512), in_=cst[:, i * 512:(i + 1) * 512]),
              writes=masks.g1(i, 0, 512), dma_key=("m", i))
    P.add("dve", lambda e: e.memset(ones.ap(0, 0, 128), 1.0), writes=ones.g1(0, 0, 128))
    act(esk.ap(0, 0, NL * 32), esk.ap(0, 0, NL * 32), AF.Exp, esk.g1(0, 0, NL * 32), esk.g1(0, 0, NL * 32))
    P.add("dve", lambda e: e.tensor_scalar(lnpa.ap(0, 0, NL * 128), lnps.ap(0, 0, NL * 128), ALPHA, None, ALU.mult),
          reads=lnps.g1(0, 0, NL * 128), writes=lnpa.g1(0, 0, NL * 128))
    posi = setupT.ap3[:, 0, :].bitcast(I32)
    P.add("sp", lambda e: e.dma_start(out=posi, in_=posr), writes=setupT.g1(0, 0, N0), dma_key="c5")
    P.add("dve", lambda e: e.tensor_copy(setupT.ap(1, 0, N0), posi), reads=setupT.g1(0, 0, N0), writes=setupT.g1(1, 0, N0))
    fq = vecs.ap(0, 0, 1)
    sg = vecs.ap(0, 1, 2)
    flag = vecs.ap(0, 2, 3)
    vg = vecs.g1(0, 0, 4)
    epsb = vecs.ap(0, 3, 4)
    C1 = 6.28125
    C2 = TWO_PI - 6.28125
    SH = 1.0 - 1e-6

    def sT(r):
        return setupT.ap(r, 0, N0)

    def sG(r):
        return setupT.g1(r, 0, N0)

    ki_ap = setupT.ap3[:, 0, :].bitcast(I32)
    ts(sT(2), sT(1), fq, None, ALU.mult, None, sG(1) + vg, sG(2))
    ts(sT(1), sT(2), 1.0 / TWO_PI, None, ALU.mult, None, sG(2), sG(1))
    P.add("dve", lambda e: e.tensor_copy(ki_ap, sT(1)), reads=sG(1), writes=sG(0))
    P.add("dve", lambda e: e.tensor_copy(sT(1), ki_ap), reads=sG(0), writes=sG(1))
    stt(sT(3), sT(1), -C1, sT(2), ALU.mult, ALU.add, sG(1) + sG(2), sG(3))
    stt(sT(3), sT(1), -C2, sT(3), ALU.mult, ALU.add, sG(1) + sG(3), sG(3))

    def fold(r, tmp):
        ts(sT(tmp), sT(r), -math.pi, 1e9, ALU.add, ALU.mult, sG(r), sG(tmp))
        ts(sT(tmp), sT(tmp), 0.0, 1.0, ALU.max, ALU.min, sG(tmp), sG(tmp))
        stt(sT(r), sT(tmp), -TWO_PI, sT(r), ALU.mult, ALU.add, sG(tmp) + sG(r), sG(r))

    fold(3, 1)
    act(sT(4), sT(3), AF.Sin, sG(3), sG(4), scale=SH)
    ts(sT(4), sT(4), sg, None, ALU.mult, None, sG(4) + vg, sG(4))
    ts(sT(2), sT(3), 0.5 * math.pi, None, ALU.add, None, sG(3), sG(2))
    fold(2, 1)
    act(sT(5), sT(2), AF.Sin, sG(2), sG(5), scale=SH)
    P.add("sp", lambda e: e.dma_start(out=ropeC_d, in_=setupT.ap(5, 0, N0)), reads=setupT.g1(5, 0, N0), writes=[("ropeC",)], dma_key="c6")
    P.add("sp", lambda e: e.dma_start(out=ropeS_d, in_=setupT.ap(4, 0, N0)), reads=setupT.g1(4, 0, N0), writes=[("ropeS",)], dma_key="c7")

    out_store_keys = []

    tiles_all = []
    for l in range(NL):
        for (g0, T) in layer_tiles(HALO * (l + 1), N0):
            tiles_all.append((l, g0, T))

    def layer_io(l):
        xin = xin0 if l == 0 else xbufs[(l - 1) % 2]
        last = (l == NL - 1)
        xo = yout if last else xbufs[l % 2]
        xo_off = -S0 if last else 0
        return xin, xo, xo_off, last

    def emit_xT_load(ti):
        l_, g0_, T_ = tiles_all[ti]
        xin_ = layer_io(l_)[0]
        xin_v_ = xin_.rearrange("(c p) t -> p c t", p=128)
        xT_ = VIEWS[ti % 2]["xT"]
        TH_ = T_ + 128
        gi0_ = g0_ - 128
        for part in range(4):
            c0, c1 = part * 8, part * 8 + 8
            dst = xT_.aps(c0, c1, 0, TH_)
            src = xin_v_[:, c0:c1, gi0_:gi0_ + TH_]
            P.add("pool", lambda e, dst=dst, src=src: e.dma_start(out=dst, in_=src),
                  reads=[("xd", id(xin_), c, (gi0_ + k * 128) // 128) for c in range(c0, c1) for k in range(TH_ // 128)],
                  writes=xT_.gr(c0, c1, 0, TH_), dma_key=("xT", part))

    emit_xT_load(0)

    for ti, (l, g0, T) in enumerate(tiles_all):
        if True:
            xin, xo, xo_off, last = layer_io(l)
            g1c = l * 128
            b1c = l * 128 + 32
            g2c = l * 128 + 64
            b2c = l * 128 + 96
            V_ = VIEWS[ti % 2]
            xT = V_["xT"]; hT = V_["hT"]; tmp4f = V_["tmp4f"]; tmp4b = V_["tmp4b"]; tmp3f = V_["tmp3f"]
            tmp3b = V_["tmp3b"]; tmpL = V_["tmpL"]; tmp1 = V_["tmp1"]; mg = V_["mg"]; x1T = V_["x1T"]
            TH = T + 128
            gi0 = g0 - 128
            dsp = S0 - g0
            special = (0 <= dsp < T)
            NB = T // 128

            P.add("sp", lambda e, TH=TH, gi0=gi0: e.dma_start(out=rC.ap(0, 0, TH), in_=ropeC_d[:, gi0:gi0 + TH]),
                  reads=[("ropeC",)], writes=rC.g1(0, 0, TH), dma_key="rC")
            P.add("sp", lambda e, TH=TH, gi0=gi0: e.dma_start(out=rS.ap(0, 0, TH), in_=ropeS_d[:, gi0:gi0 + TH]),
                  reads=[("ropeS",)], writes=rS.g1(0, 0, TH), dma_key="rS")

            t1rr = [0]

            def t1slot():
                s = t1rr[0]
                t1rr[0] = (s + 1) % 12
                return s

            pend = []

            def flush():
                for f in pend:
                    f()
                del pend[:]

            def rope(bank, c0t, n, dst_lt, dst_r, dst_c0):
                a = t1slot(); b_ = t1slot(); c_ = t1slot()
                pb = psb[bank]
                act(tmp1.ap(a, 0, n), pb[:, 0:n], AF.Copy, [PS(bank)], tmp1.g1(a, 0, n))

                tmp1_ = tmp1

                def part2():
                    bk2 = next_bank()
                    rhs_ap = tmp1_.ap(a, 0, n)
                    P.add("pe", lambda e: e.matmul(psb[bk2][:, 0:n], perm.ap(0, 0, 128), rhs_ap, start=True, stop=True),
                          reads=perm.g1(0, 0, 128) + tmp1_.g1(a, 0, n), writes=[PS(bk2)])
                    tt(tmp1.ap(b_, 0, n), tmp1.ap(a, 0, n), rC.ap(0, c0t, c0t + n), ALU.mult,
                       tmp1.g1(a, 0, n) + rC.g1(0, c0t, c0t + n), tmp1.g1(b_, 0, n))
                    tt(tmp1.ap(c_, 0, n), psb[bk2][:, 0:n], rS.ap(0, c0t, c0t + n), ALU.mult,
                       [PS(bk2)] + rS.g1(0, c0t, c0t + n), tmp1.g1(c_, 0, n))
                    tt(dst_lt.ap(dst_r, dst_c0, dst_c0 + n), tmp1.ap(b_, 0, n), tmp1.ap(c_, 0, n), ALU.add,
                       tmp1.g1(b_, 0, n) + tmp1.g1(c_, 0, n), dst_lt.g1(dst_r, dst_c0, dst_c0 + n))
                pend.append(part2)

            wbase = l * N_WIN
            for r in range(16):
                s = load_slab(win[wbase + r], 32)
                bk = next_bank()
                for kc in range(32):
                    mm(psb[bk][:, 0:T], wsl(s, kc), xT.ap(kc, 128, TH), kc == 0, kc == 31,
                       [("w", s)] + xT.g1(kc, 128, TH), bk)
                flush()
                rope(bk, 128, T, qT, r, 0)
            for r in range(4):
                s = load_slab(win[wbase + 16 + r], 32)
                bk = next_bank()
                for kc in range(32):
                    mm(psb[bk][:, 0:T], wsl(s, kc), xT.ap(kc, 128, TH), kc == 0, kc == 31,
                       [("w", s)] + xT.g1(kc, 128, TH), bk)
                bkh = next_bank()
                for kc in range(32):
                    mm(psb[bkh][:, 0:128], wsl(s, kc), xT.ap(kc, 0, 128), kc == 0, kc == 31,
                       [("w", s)] + xT.g1(kc, 0, 128), bkh)
                flush()
                rope(bk, 128, T, kT, r, 128)
                rope(bkh, 0, 128, kT, r, 0)
            for c in range(8):
                cwb = l * 24 + c * 3
                w0 = cws.ap(0, cwb, cwb + 1); w1 = cws.ap(0, cwb + 1, cwb + 2); w2 = cws.ap(0, cwb + 2, cwb + 3)
                cg = cws.g1(0, cwb, cwb + 3)
                s = load_slab(win[wbase + 20 + c * 3 + 0], 32)
                bka = next_bank()
                for kc in range(32):
                    mm(psb[bka][:, 0:T], wsl(s, kc), xT.ap(kc, 128, TH), kc == 0, kc == 31, [("w", s)] + xT.g1(kc, 128, TH), bka)
                for kc in range(32):
                    mm(psb[6][:, 0:2], wsl(s, kc), xT.ap(kc, 126, 128), kc == 0, kc == 31, [("w", s)] + xT.g1(kc, 126, 128), 6)
                flush()
                ta = t1slot(); tu = t1slot(); tb = t1slot(); tc_ = t1slot()
                act(tmp1.ap(ta, 2, 2 + T), psb[bka][:, 0:T], AF.Copy, [PS(bka)], tmp1.g1(ta, 2, 2 + T))
                if special and dsp == 0:
                    act(tmp1.ap(ta, 0, 2), psb[6][:, 0:2], AF.Identity, [PS(6)] + vg, tmp1.g1(ta, 0, 2), scale=flag)
                else:
                    act(tmp1.ap(ta, 0, 2), psb[6][:, 0:2], AF.Copy, [PS(6)], tmp1.g1(ta, 0, 2))
                    if special:
                        act(tmp1.ap(ta, dsp, dsp + 2), tmp1.ap(ta, dsp, dsp + 2), AF.Identity,
                            tmp1.g1(ta, dsp, dsp + 2) + vg, tmp1.g1(ta, dsp, dsp + 2), scale=flag)
                s = load_slab(win[wbase + 20 + c * 3 + 1], 32)
                bkb = next_bank()
                for kc in range(32):
                    mm(psb[bkb][:, 0:T], wsl(s, kc), xT.ap(kc, 128, TH), kc == 0, kc == 31, [("w", s)] + xT.g1(kc, 128, TH), bkb)
                for kc in range(32):
                    mm(psb[7][:, 0:2], wsl(s, kc), xT.ap(kc, 126, 128), kc == 0, kc == 31, [("w", s)] + xT.g1(kc, 126, 128), 7)
                tt(tmp1.ap(tu, 2, 2 + T), tmp1.ap(ta, 2, 2 + T), psb[bkb][:, 0:T], ALU.mult,
                   tmp1.g1(ta, 2, 2 + T) + [PS(bkb)], tmp1.g1(tu, 2, 2 + T))
                tt(tmp1.ap(tu, 0, 2), tmp1.ap(ta, 0, 2), psb[7][:, 0:2], ALU.mult,
                   tmp1.g1(ta, 0, 2) + [PS(7)], tmp1.g1(tu, 0, 2))
                ts(tmp1.ap(tb, 0, T), tmp1.ap(tu, 0, T), w0, None, ALU.mult, None, tmp1.g1(tu, 0, T) + cg, tmp1.g1(tb, 0, T))
                stt(tmp1.ap(tc_, 0, T), tmp1.ap(tu, 1, 1 + T), w1, tmp1.ap(tb, 0, T), ALU.mult, ALU.add,
                    tmp1.g1(tu, 1, 1 + T) + tmp1.g1(tb, 0, T) + cg, tmp1.g1(tc_, 0, T))
                stt(tmp1.ap(tb, 0, T), tmp1.ap(tu, 2, 2 + T), w2, tmp1.ap(tc_, 0, T), ALU.mult, ALU.add,
                    tmp1.g1(tu, 2, 2 + T) + tmp1.g1(tc_, 0, T) + cg, tmp1.g1(tb, 0, T))
                s = load_slab(win[wbase + 20 + c * 3 + 2], 32)
                bkc = next_bank()
                for kc in range(32):
                    mm(psb[bkc][:, 0:T], wsl(s, kc), xT.ap(kc, 128, TH), kc == 0, kc == 31, [("w", s)] + xT.g1(kc, 128, TH), bkc)
                tt(yconv.ap(c, 0, T), tmp1.ap(tb, 0, T), psb[bkc][:, 0:T], ALU.mult,
                   tmp1.g1(tb, 0, T) + [PS(bkc)], yconv.g1(c, 0, T))
            for j in range(4):
                s = load_slab(win[wbase + 44 + j], 32)
                for blk in range(NB + 1):
                    bk = next_bank()
                    po = psb[bk][:, 0:128]
                    for kc in range(32):
                        mm(po, xT.ap(kc, blk * 128, blk * 128 + 128), wsl(s, kc), kc == 0, kc == 31,
                           [("w", s)] + xT.g1(kc, blk * 128, blk * 128 + 128), bk)
                    act(Vt.ap(blk, j * 128, j * 128 + 128), po, AF.Copy, [PS(bk)], Vt.g1(blk, j * 128, j * 128 + 128))
            flush()
            abank = [0]

            def next_abank():
                b = abank[0]
                abank[0] = (b + 1) % 8
                return b

            ptrr = [0]

            def stage_a(n, g, half):
                p0, p1 = half * 64, half * 64 + 64
                pts = []
                for kb in range(2):
                    kc0 = (n + kb) * 128
                    bk = next_abank()
                    P.add("pe", lambda e, bk=bk, kc0=kc0:
                          e.matmul(psb[bk][:, 0:512], kT.ap(g, kc0, kc0 + 128, p0, p1),
                                   qT.aps(4 * g, 4 * g + 4, n * 128, n * 128 + 128, p0, p1), start=True, stop=True),
                          reads=kT.g1(g, kc0, kc0 + 128) + qT.gr(4 * g, 4 * g + 4, n * 128, n * 128 + 128),
                          writes=[PS(bk)])
                    pi = ptrr[0]
                    ptrr[0] = (pi + 1) % 4
                    act(PT.ap(pi, 0, 512), psb[bk][:, 0:512], AF.Exp, [PS(bk)], PT.g1(pi, 0, 512), scale=0.125)
                    if kb == 1:
                        mi = 1
                    else:
                        mi = 2 if (special and n * 128 == dsp) else 0
                    tt(PT.ap(pi, 0, 512), PT.ap(pi, 0, 512), masks.ap(mi, 0, 512), ALU.mult,
                       PT.g1(pi, 0, 512) + masks.g1(mi, 0, 512), PT.g1(pi, 0, 512))
                    pts.append(pi)
                return pts

            def stage_b(n, g, half, pts):
                p0, p1 = half * 64, half * 64 + 64
                bo = next_abank()
                for kb in range(2):
                    P.add("pe", lambda e, kb=kb, pi=pts[kb]:
                          e.matmul(psb[bo][:, 0:512], Vt.ap(n + kb, g * 128, g * 128 + 128), PT.ap(pi, 0, 512),
                                   start=(kb == 0), stop=(kb == 1)),
                          reads=Vt.g1(n + kb, g * 128, g * 128 + 128) + PT.g1(pts[kb], 0, 512), writes=[PS(bo)])
                bd = next_abank()
                for kb in range(2):
                    P.add("pe", lambda e, kb=kb, pi=pts[kb]:
                          e.matmul(psb[bd][:, 0:512], ones.ap(0, 0, 128), PT.ap(pi, 0, 512),
                                   start=(kb == 0), stop=(kb == 1)),
                          reads=ones.g1(0, 0, 128) + PT.g1(pts[kb], 0, 512), writes=[PS(bd)])
                rt = t1slot()
                for h in range(4):
                    hd = l * 32 + (8 * g + 4 * half + h)
                    act(tmp1.ap(rt, h * 128, h * 128 + 128, p0, p1), psb[bd][p0:p1, h * 128:h * 128 + 128], AF.Ln,
                        [PS(bd)] + esk.g1(0, hd, hd + 1), tmp1.g1(rt, h * 128, h * 128 + 128),
                        bias=esk.ap(0, hd, hd + 1, p0, p1), scale=1.0)
                act(tmp1.ap(rt, 0, 512, p0, p1), tmp1.ap(rt, 0, 512, p0, p1), AF.Exp, tmp1.g1(rt, 0, 512), tmp1.g1(rt, 0, 512), scale=-1.0)
                rec_ap = tmp1.ap3[p0:p1, rt, 0:512].rearrange("p (h q) -> p h q", q=128)
                P.add("dve", lambda e:
                      e.tensor_tensor(aT.aps(4 * g, 4 * g + 4, n * 128, n * 128 + 128, p0, p1),
                                      psb[bo][p0:p1, 0:512].rearrange("p (h q) -> p h q", q=128),
                                      rec_ap, ALU.mult),
                      reads=[PS(bo)] + tmp1.g1(rt, 0, 512),
                      writes=aT.gr(4 * g, 4 * g + 4, n * 128, n * 128 + 128))

            its = [(n, g, half) for n in range(NB) for g in range(4) for half in range(2)]
            prev = None
            for it in its:
                pts = stage_a(*it)
                if prev is not None:
                    stage_b(*prev)
                prev = it + (pts,)
            stage_b(*prev)
            t2rr = 0
            for m in range(32):
                s = load_slab(win[wbase + 48 + 2 * m], 32)
                bg = next_bank()
                for kc in range(32):
                    mm(psb[bg][:, 0:T], wsl(s, kc), xT.ap(kc, 128, TH), kc == 0, kc == 31, [("w", s)] + xT.g1(kc, 128, TH), bg)
                ia = t2rr; ib = t2rr + 1; ic = t2rr + 2; idd = t2rr + 3
                t2rr = (t2rr + 4) % 8
                act(tmp2.ap(ia, 0, T), psb[bg][:, 0:T], AF.Sigmoid, [PS(bg)], tmp2.g1(ia, 0, T))
                s = load_slab(win[wbase + 48 + 2 * m + 1], 32)
                bg2 = next_bank()
                for kc in range(32):
                    mm(psb[bg2][:, 0:T], wsl(s, kc), xT.ap(kc, 128, TH), kc == 0, kc == 31, [("w", s)] + xT.g1(kc, 128, TH), bg2)
                act(tmp2.ap(ib, 0, T), psb[bg2][:, 0:T], AF.Sigmoid, [PS(bg2)], tmp2.g1(ib, 0, T))
                s = load_slab(wbr[l * N_WBR + m], 24)
                ba = next_bank()
                for kc in range(16):
                    mm(psb[ba][:, 0:T], wsl(s, kc), aT.ap(kc, 0, T), kc == 0, kc == 15, [("w", s)] + aT.g1(kc, 0, T), ba)
                bc = next_bank()
                for kc in range(8):
                    mm(psb[bc][:, 0:T], wsl(s, 16 + kc), yconv.ap(kc, 0, T), kc == 0, kc == 7, [("w", s)] + yconv.g1(kc, 0, T), bc)
                tt(tmp2.ap(ic, 0, T), tmp2.ap(ia, 0, T), psb[ba][:, 0:T], ALU.mult, tmp2.g1(ia, 0, T) + [PS(ba)], tmp2.g1(ic, 0, T))
                tt(tmp2.ap(idd, 0, T), tmp2.ap(ib, 0, T), psb[bc][:, 0:T], ALU.mult, tmp2.g1(ib, 0, T) + [PS(bc)], tmp2.g1(idd, 0, T))
                tt(mg.ap(m, 0, T), tmp2.ap(ic, 0, T), tmp2.ap(idd, 0, T), ALU.add,
                   tmp2.g1(ic, 0, T) + tmp2.g1(idd, 0, T), mg.g1(m, 0, T))

            def stats(m, ybf_lt, ybf_r, ysq_r):
                act(ybf_lt.ap(ybf_r, 0, T), y.ap(m, 0, T), AF.Copy, y.g1(m, 0, T), ybf_lt.g1(ybf_r, 0, T))
                act(ybf_lt.ap(ysq_r, 0, T), y.ap(m, 0, T), AF.Square, y.g1(m, 0, T), ybf_lt.g1(ysq_r, 0, T))

                def part2():
                    mm(psb[6][:, 0:T], ones.ap(0, 0, 128), ybf_lt.ap(ybf_r, 0, T), m == 0, m == 31,
                       ones.g1(0, 0, 128) + ybf_lt.g1(ybf_r, 0, T), 6)
                    mm(psb[7][:, 0:T], ones.ap(0, 0, 128), ybf_lt.ap(ysq_r, 0, T), m == 0, m == 31,
                       ones.g1(0, 0, 128) + ybf_lt.g1(ysq_r, 0, T), 7)
                pend.append(part2)

            def ln_finish():
                ts(lnt.ap(2, 0, T), psb[6][:, 0:T], 1.0 / D, None, ALU.mult, None, [PS(6)], lnt.g1(2, 0, T))
                tt(lnt.ap(1, 0, T), lnt.ap(2, 0, T), lnt.ap(2, 0, T), ALU.mult, lnt.g1(2, 0, T), lnt.g1(1, 0, T))
                stt(lnt.ap(0, 0, T), psb[7][:, 0:T], 1.0 / D, lnt.ap(1, 0, T), ALU.mult, ALU.subtract,
                    [PS(7)] + lnt.g1(1, 0, T), lnt.g1(0, 0, T))
                act(lnt.ap(0, 0, T), lnt.ap(0, 0, T), AF.Sqrt, lnt.g1(0, 0, T) + vg, lnt.g1(0, 0, T), bias=epsb, scale=1.0)
                P.add("dve", lambda e: e.reciprocal(lnt.ap(0, 0, T), lnt.ap(0, 0, T)), reads=lnt.g1(0, 0, T), writes=lnt.g1(0, 0, T))
                stt(lnt.ap(1, 0, T), lnt.ap(2, 0, T), -1.0, lnt.ap(0, 0, T), ALU.mult, ALU.mult,
                    lnt.g1(2, 0, T) + lnt.g1(0, 0, T), lnt.g1(1, 0, T))

            for m in range(32):
                s = load_slab(wo[l * N_WO + m], 32)
                bk = next_bank()
                for kc in range(32):
                    mm(psb[bk][:, 0:T], wsl(s, kc), mg.ap(kc, 0, T), kc == 0, kc == 31, [("w", s)] + mg.g1(kc, 0, T), bk)
                flush()
                xs = m % 4
                xr_dst = tmp3f.ap(xs, 0, T)
                P.add("sp", lambda e, xr_dst=xr_dst, m=m, g0=g0, T=T, xin=xin:
                      e.dma_start(out=xr_dst, in_=xin[m * 128:(m + 1) * 128, g0:g0 + T]),
                      reads=[("xd", id(xin), m, (g0 + k * 128) // 128) for k in range(T // 128)],
                      writes=tmp3f.g1(xs, 0, T), dma_key=("xr", xs))
                stt(y.ap(m, 0, T), tmp3f.ap(xs, 0, T), ALPHA, psb[bk][:, 0:T], ALU.mult, ALU.add,
                    tmp3f.g1(xs, 0, T) + [PS(bk)], y.g1(m, 0, T))
                stats(m, tmp3b, (m % 2) * 2, (m % 2) * 2 + 1)
            flush()
            ln_finish()
            for m in range(32):
                ta = 4 + (m % 2) * 2
                tb = ta + 1
                tt(tmp3f.ap(ta, 0, T), y.ap(m, 0, T), lnt.ap(0, 0, T), ALU.mult, y.g1(m, 0, T) + lnt.g1(0, 0, T), tmp3f.g1(ta, 0, T))
                tt(tmp3f.ap(tb, 0, T), tmp3f.ap(ta, 0, T), lnt.ap(1, 0, T), ALU.add, tmp3f.g1(ta, 0, T) + lnt.g1(1, 0, T), tmp3f.g1(tb, 0, T))
                act(x1T.ap(m, 0, T), tmp3f.ap(tb, 0, T), AF.Identity, tmp3f.g1(tb, 0, T) + lnps.g1(0, g1c + m, g1c + m + 1) + lnps.g1(0, b1c + m, b1c + m + 1),
                    x1T.g1(m, 0, T), bias=lnps.ap(0, b1c + m, b1c + m + 1), scale=lnps.ap(0, g1c + m, g1c + m + 1))
                act(y.ap(m, 0, T), tmp3f.ap(tb, 0, T), AF.Identity, tmp3f.g1(tb, 0, T) + lnpa.g1(0, g1c + m, g1c + m + 1) + lnpa.g1(0, b1c + m, b1c + m + 1),
                    y.g1(m, 0, T), bias=lnpa.ap(0, b1c + m, b1c + m + 1), scale=lnpa.ap(0, g1c + m, g1c + m + 1))

            for gq in range(4):
                for j in range(32):
                    s = load_slab(wup[l * N_WUP + gq * 32 + j], 32)
                    bk = next_bank()
                    for kc in range(32):
                        mm(psb[bk][:, 0:T], wsl(s, kc), x1T.ap(kc, 0, T), kc == 0, kc == 31, [("w", s)] + x1T.g1(kc, 0, T), bk)
                    tr = j % 2
                    act(tmp4f.ap(tr, 0, T), psb[bk][:, 0:T], AF.Relu, [PS(bk)], tmp4f.g1(tr, 0, T))
                    tt(hT.ap(j, 0, T), tmp4f.ap(tr, 0, T), tmp4f.ap(tr, 0, T), ALU.mult, tmp4f.g1(tr, 0, T), hT.g1(j, 0, T))
                for m in range(32):
                    s = load_slab(wdn[l * N_WDN + gq * 32 + m], 32)
                    if gq == 3 and m == 6 and ti + 1 < len(tiles_all):
                        emit_xT_load(ti + 1)
                    bk = next_bank()
                    for kc in range(32):
                        mm(psb[bk][:, 0:T], wsl(s, kc), hT.ap(kc, 0, T), kc == 0, kc == 31, [("w", s)] + hT.g1(kc, 0, T), bk)
                    flush()
                    tt(y.ap(m, 0, T), y.ap(m, 0, T), psb[bk][:, 0:T], ALU.add, y.g1(m, 0, T) + [PS(bk)], y.g1(m, 0, T))
                    if gq == 3:
                        stats(m, tmp4b, (m % 2) * 2, (m % 2) * 2 + 1)
            flush()
            ln_finish()
            for m in range(32):
                ta = (m % 2) * 2
                tb = ta + 1
                os_ = m % 2
                tt(tmpL.ap(ta, 0, T), y.ap(m, 0, T), lnt.ap(0, 0, T), ALU.mult, y.g1(m, 0, T) + lnt.g1(0, 0, T), tmpL.g1(ta, 0, T))
                tt(tmpL.ap(tb, 0, T), tmpL.ap(ta, 0, T), lnt.ap(1, 0, T), ALU.add, tmpL.g1(ta, 0, T) + lnt.g1(1, 0, T), tmpL.g1(tb, 0, T))
                act(ostage.ap(os_, 0, T), tmpL.ap(tb, 0, T), AF.Identity,
                    tmpL.g1(tb, 0, T) + lnps.g1(0, g2c + m, g2c + m + 1) + lnps.g1(0, b2c + m, b2c + m + 1),
                    ostage.g1(os_, 0, T), bias=lnps.ap(0, b2c + m, b2c + m + 1), scale=lnps.ap(0, g2c + m, g2c + m + 1))
                lo = g0 + xo_off
                c_lo = 0
                if lo < 0:
                    c_lo = -lo
                if c_lo >= T:
                    continue
                wres = [("xd", id(xo), m, (g0 + k * 128) // 128) for k in range(T // 128)]
                P.add("sp", lambda e, os_=os_, m=m, lo=lo, c_lo=c_lo, T=T, xo=xo:
                      e.dma_start(out=xo[m * 128:(m + 1) * 128, lo + c_lo:lo + T], in_=ostage.ap(os_, c_lo, T)),
                      reads=ostage.g1(os_, 0, T), writes=wres, dma_key=("os", os_))
                if last:
                    out_store_keys.extend(wres)

    P.add("sp", None, reads=list(dict.fromkeys(out_store_keys)))
    P.emit(nc, stack)
    stack.close()
    return nc


Q0, K0, V0, CB0, CC0, CH0, GA0, GC0 = 0, 2048, 2304, 2560, 3584, 4608, 5632, 9728


def _head_order():
    rows = []
    for r in range(16):
        g, i = r // 4, r % 4
        rows.append((8 * g + i, 8 * g + 4 + i))
    return rows


def _win_cols():
    cols = []
    for (ha, hb) in _head_order():
        cols.append(np.concatenate([Q0 + ha * 64 + np.arange(64), Q0 + hb * 64 + np.arange(64)]))
    for g in range(4):
        k = K0 + g * 64 + np.arange(64)
        cols.append(np.concatenate([k, k]))
    for c in range(8):
        cols.append(CC0 + c * 128 + np.arange(128))
        cols.append(CH0 + c * 128 + np.arange(128))
        cols.append(CB0 + c * 128 + np.arange(128))
    for j in range(4):
        v = V0 + j * 64 + np.arange(64)
        cols.append(np.concatenate([v, v]))
    for m in range(32):
        cols.append(GA0 + m * 128 + np.arange(128))
        cols.append(GC0 + m * 128 + np.arange(128))
    return np.concatenate(cols)


def _slabs(W, cols=None, rows=None):
    if rows is not None:
        W = W[rows]
    if cols is not None:
        W = W[:, cols]
    K, N = W.shape
    return np.ascontiguousarray(W.reshape(K // 128, 128, N // 128, 128).transpose(2, 1, 0, 3)).reshape(N // 128, 128, K)


def _prep_layer(w_in, w_br_attn, w_br_conv, w_o, w_up, w_down):
    win = _slabs(w_in, cols=_win_cols())
    arow = np.concatenate([np.concatenate([ha * 64 + np.arange(64), hb * 64 + np.arange(64)]) for (ha, hb) in _head_order()])
    wbr = np.concatenate([_slabs(w_br_attn, rows=arow), _slabs(w_br_conv)], axis=2)
    wo = _slabs(w_o)
    wup = _slabs(w_up)
    wd = w_down.reshape(4, 32, 128, 32, 128).transpose(0, 3, 2, 1, 4)
    wdn = np.ascontiguousarray(wd).reshape(128, 128, 4096)
    return win, wbr, wo, wup, wdn


def _consts():
    j = np.arange(128)[:, None]
    i = np.arange(128)[None, :]
    mp = (j > i).astype(np.float32)
    mc = (j <= i).astype(np.float32)
    perm = np.zeros((128, 128), np.float32)
    fq = np.zeros(128, np.float32)
    sg = np.zeros(128, np.float32)
    inv_freq = (ROPE_THETA ** (-np.arange(0, 16, 2, dtype=np.float32) / np.float32(16))).astype(np.float32)
    for m in range(128):
        d = m % 64
        if d < 8:
            perm[m + 8, m] = 1.0
            fq[m] = inv_freq[d]
            sg[m] = -1.0
        elif d < 16:
            perm[m - 8, m] = 1.0
            fq[m] = inv_freq[d - 8]
            sg[m] = 1.0
    return np.tile(mp, (1, 4)), np.tile(mc, (1, 4)), perm, fq, sg


_PROG_CACHE = {}


def _get_prog(NL):
    if NL not in _PROG_CACHE:
        _PROG_CACHE[NL] = build_program(NL)
    return _PROG_CACHE[NL]


def _small_params(layers, conv_w, attn_sinks, ln1_g, ln1_b, ln2_g, ln2_b):
    hord = np.array([h for pair in _head_order() for h in pair])
    lnp = np.concatenate([np.concatenate([a[l].reshape(32, 128).T for a in (ln1_g, ln1_b, ln2_g, ln2_b)], axis=1) for l in layers], axis=1)
    cwp = np.concatenate([conv_w[l].reshape(3, 8, 128).transpose(2, 1, 0).reshape(128, 24) for l in layers], axis=1)
    snk = np.concatenate([np.tile(attn_sinks[l][None, :], (128, 1)) for l in layers], axis=1)
    return (np.ascontiguousarray(lnp, dtype=np.float32), np.ascontiguousarray(cwp, dtype=np.float32),
            np.ascontiguousarray(snk, dtype=np.float32))


def kernel(x, positions, w_in, conv_w, attn_sinks, w_br_attn, w_br_conv, w_o,
           ln1_g, ln1_b, w_up, w_down, ln2_g, ln2_b):
    x = np.asarray(x); positions = np.asarray(positions)
    B, S, _ = x.shape
    cps = NCORES // B
    mp4, mc4, perm, fq, sg = _consts()
    zeros4 = np.zeros_like(mp4)

    def run_layers(xT_full, layers):
        NL = len(layers)
        N0 = TOK_CORE + HALO * NL
        nc = _get_prog(NL)
        preps = [_prep_layer(np.asarray(w_in[l]), np.asarray(w_br_attn[l]), np.asarray(w_br_conv[l]),
                             np.asarray(w_o[l]), np.asarray(w_up[l]), np.asarray(w_down[l])) for l in layers]
        win = np.concatenate([p[0] for p in preps], axis=0)
        wbr = np.concatenate([p[1] for p in preps], axis=0)
        wo = np.concatenate([p[2] for p in preps], axis=0)
        wup = np.concatenate([p[3] for p in preps], axis=0)
        wdn = np.concatenate([p[4] for p in preps], axis=0)
        del preps
        lnp, cwp, snk = _small_params(layers, np.asarray(conv_w), np.asarray(attn_sinks), np.asarray(ln1_g),
                                      np.asarray(ln1_b), np.asarray(ln2_g), np.asarray(ln2_b))
        in_maps = []
        for c in range(NCORES):
            b, q = c // cps, c % cps
            s0 = q * TOK_CORE
            h = HALO * NL
            xin = np.zeros((D, N0), np.float32)
            pos = np.zeros((N0,), np.int32)
            lo = max(0, s0 - h)
            xin[:, h - (s0 - lo):] = xT_full[b][:, lo:s0 + TOK_CORE]
            pos[h - (s0 - lo):] = positions[b, lo:s0 + TOK_CORE]
            first = (q == 0)
            cst = np.concatenate([mp4, mc4, zeros4 if first else mp4], axis=1).astype(np.float32)
            vec = np.stack([fq, sg, np.full(128, 0.0 if first else 1.0, np.float32), np.full(128, LN_EPS, np.float32)], axis=1)
            in_maps.append({
                "xin": xin, "posr": np.ascontiguousarray(np.tile(pos[None, :], (128, 1))),
                "win": win, "wbr": wbr, "wo": wo, "wup": wup, "wdn": wdn,
                "lnp": lnp, "cwp": cwp, "snk": snk, "cst": cst, "perm": perm,
                "vec": np.ascontiguousarray(vec, dtype=np.float32),
            })
        res = run_bass_kernel_spmd(nc, in_maps, core_ids=list(range(NCORES)))
        out = np.empty_like(xT_full)
        for c in range(NCORES):
            b, q = c // cps, c % cps
            out[b][:, q * TOK_CORE:(q + 1) * TOK_CORE] = res.results[c]["yout"]
        return out

    xT_full = np.ascontiguousarray(np.transpose(x, (0, 2, 1)))
    if FUSED:
        xT_full = run_layers(xT_full, list(range(DEPTH)))
    else:
        for l in range(DEPTH):
            xT_full = run_layers(xT_full, [l])
    return np.ascontiguousarray(np.transpose(xT_full, (0, 2, 1))).astype(np.float32)
```

```python
import contextlib
import math
import numpy as np
import concourse.bass as bass
import concourse.mybir as mybir
from concourse.bass_utils import run_bass_kernel_spmd

F32 = mybir.dt.float32
BF16 = mybir.dt.bfloat16
I32 = mybir.dt.int32
AF = mybir.ActivationFunctionType
ALU = mybir.AluOpType

D = 4096
NCH = 32
DEPTH = 4
NCORES = 8
TOK_CORE = 2048
HALO = 128
ALPHA = (2 * DEPTH) ** 0.25
LN_EPS = 1e-5
ROPE_THETA = 500000.0
TWO_PI = 2.0 * math.pi
NW = 5
SAME_ENGINE_SYNC = True
FUSED = True

N_WIN = 112
N_WBR = 32
N_WO = 32
N_WUP = 128
N_WDN = 128


class Op:
    __slots__ = ("eng", "fn", "deps", "marked", "count", "sig", "dma")

    def __init__(self, eng, fn, sig, dma):
        self.eng = eng
        self.fn = fn
        self.deps = ()
        self.marked = False
        self.count = 0
        self.sig = sig
        self.dma = dma


class Prog:
    ENGS = ("pe", "act", "dve", "pool", "sp")

    def __init__(self):
        self.ops = []
        self.eng_ops = {e: [] for e in self.ENGS}
        self.lastw = {}
        self.readers = {}
        self.dma_last = {}
        self.dma_n = {}

    def add(self, eng, fn, reads=(), writes=(), dma_key=None):
        oid = len(self.ops)
        sig = eng if dma_key is None else ("dma", dma_key)
        op = Op(eng, fn, sig, dma_key is not None)
        deps = {}
        lastw = self.lastw
        readers = self.readers
        ops = self.ops

        def need(d):
            s = ops[d].sig
            if s == eng and (eng == "pe" or not SAME_ENGINE_SYNC):
                return
            if deps.get(s, -1) < d:
                deps[s] = d

        for r in reads:
            w = lastw.get(r)
            if w is not None:
                need(w)
        for r in writes:
            w = lastw.get(r)
            if w is not None:
                need(w)
            rd = readers.get(r)
            if rd:
                for d in rd.values():
                    need(d)
        if dma_key is not None:
            p = self.dma_last.get(dma_key)
            if p is not None:
                need(p)
            self.dma_last[dma_key] = oid
            n = self.dma_n.get(dma_key, 0) + 1
            self.dma_n[dma_key] = n
            op.count = 16 * n
            op.marked = True
        for r in writes:
            lastw[r] = oid
            readers[r] = {}
        for r in reads:
            rd = readers.get(r)
            if rd is None:
                rd = readers[r] = {}
            rd[sig] = oid
        op.deps = tuple(deps.values())
        for d in op.deps:
            ops[d].marked = True
        ops.append(op)
        self.eng_ops[eng].append(oid)
        return oid

    def emit(self, nc, stack):
        ops = self.ops
        sems = {}
        for e in self.ENGS:
            sems[e] = stack.enter_context(nc.semaphore("s_" + e))
            c = 0
            for oid in self.eng_ops[e]:
                op = ops[oid]
                if not op.dma and op.marked:
                    c += 1
                    op.count = c
            assert c < 60000, (e, c)
        for i, k in enumerate(self.dma_n):
            assert 16 * self.dma_n[k] < 60000, (k, self.dma_n[k])
            sems[("dma", k)] = stack.enter_context(nc.semaphore("d%d" % i))
        block = stack.enter_context(nc.Block())

        def run(e_name, eng):
            waited = {}
            for oid in self.eng_ops[e_name]:
                op = ops[oid]
                for d in op.deps:
                    dop = ops[d]
                    if waited.get(dop.sig, 0) < dop.count:
                        eng.wait_ge(sems[dop.sig], dop.count)
                        waited[dop.sig] = dop.count
                if op.fn is None:
                    continue
                ins = op.fn(eng)
                if op.dma:
                    ins.then_inc(sems[op.sig], 16)
                elif op.marked:
                    ins.then_inc(sems[op.sig], 1)

        @block.tensor
        def _(eng):
            run("pe", eng)

        @block.scalar
        def _(eng):
            run("act", eng)

        @block.vector
        def _(eng):
            run("dve", eng)

        @block.gpsimd
        def _(eng):
            run("pool", eng)

        @block.sync
        def _(eng):
            run("sp", eng)


class Region:
    def __init__(self, nc, stack, name, nbytes):
        self.name = name
        self.nbytes = nbytes
        self.t = stack.enter_context(nc.sbuf_tensor(name, [128, nbytes // 4], F32))


class LT:
    def __init__(self, reg, off, dt, rows, cols):
        self.reg = reg
        self.off = off
        self.esz = 2 if dt == BF16 else 4
        self.rows = rows
        self.cols = cols
        nb = rows * cols * self.esz
        assert off % 4 == 0 and nb % 4 == 0 and off + nb <= reg.nbytes, (reg.name, off, nb)
        a = reg.t[:, off // 4:(off + nb) // 4]
        if dt != F32:
            a = a.bitcast(dt)
        self.ap3 = a.rearrange("p (r c) -> p r c", c=cols)

    def ap(self, r, c0, c1, p0=0, p1=128):
        return self.ap3[p0:p1, r, c0:c1]

    def aps(self, r0, r1, c0, c1, p0=0, p1=128):
        return self.ap3[p0:p1, r0:r1, c0:c1]

    def gr(self, r0, r1, c0, c1):
        out = []
        nm = self.reg.name
        for r in range(r0, r1):
            b0 = self.off + (r * self.cols + c0) * self.esz
            b1 = self.off + (r * self.cols + c1) * self.esz
            for g in range(b0 // 256, (b1 + 255) // 256):
                out.append((nm, g))
        return out

    def g1(self, r, c0, c1):
        return self.gr(r, r + 1, c0, c1)


def layer_tiles(out_start, n_end):
    nb = (n_end - out_start) // 128
    nt = (nb + 3) // 4
    base, extra = nb // nt, nb % nt
    sizes = [base] * (nt - extra) + [base + 1] * extra
    tiles = []
    g = out_start
    for b in sizes:
        tiles.append((g, b * 128))
        g += b * 128
    assert g == n_end
    return tiles


def build_program(NL):
    N0 = TOK_CORE + HALO * NL
    S0 = HALO * NL
    nc = bass.Bass("TRN2", target_bir_lowering=False)
    P = Prog()
    stack = contextlib.ExitStack()

    def din(name, shape, dt=F32):
        return nc.dram_tensor(name, list(shape), dt, kind="ExternalInput").ap()

    xin0 = din("xin", [D, N0])
    posr = din("posr", [128, N0], I32)
    win = din("win", [NL * N_WIN, 128, 4096])
    wbr = din("wbr", [NL * N_WBR, 128, 3072])
    wo = din("wo", [NL * N_WO, 128, 4096])
    wup = din("wup", [NL * N_WUP, 128, 4096])
    wdn = din("wdn", [NL * N_WDN, 128, 4096])
    lnp = din("lnp", [128, NL * 4 * 32])
    cwp = din("cwp", [128, NL * 24])
    snk = din("snk", [128, NL * 32])
    cst = din("cst", [128, 3 * 512])
    perm_d = din("perm", [128, 128])
    vec = din("vec", [128, 4])
    yout = nc.dram_tensor("yout", [D, TOK_CORE], F32, kind="ExternalOutput").ap()
    ropeC_d = nc.dram_tensor("ropeC", [128, N0], F32, kind="Internal").ap()
    ropeS_d = nc.dram_tensor("ropeS", [128, N0], F32, kind="Internal").ap()
    xbufs = [nc.dram_tensor("xs%d" % i, [D, N0], F32, kind="Internal").ap() for i in range(2)] if NL > 1 else []

    RA = Region(nc, stack, "RA", 65536)
    RB = Region(nc, stack, "RB", 40960)
    RC = Region(nc, stack, "RC", 40960)
    RW = Region(nc, stack, "RW", NW * 8192)
    RM = Region(nc, stack, "RM", 19200)
    psb = [stack.enter_context(nc.psum_tensor("ps%d" % i, [128, 512], F32)) for i in range(8)]

    y = LT(RA, 0, F32, 32, 512)
    qT = LT(RA, 0, BF16, 16, 512)
    aT = LT(RA, 16384, BF16, 16, 512)
    yconv = LT(RA, 32768, BF16, 8, 512)
    kT = LT(RA, 40960, BF16, 4, 640)
    Vt = LT(RA, 46080, BF16, 5, 512)
    PT = LT(RA, 51200, BF16, 4, 512)
    rC = LT(RA, 55296, F32, 1, 640)
    rS = LT(RA, 57856, F32, 1, 640)
    tmp2 = LT(RA, 0, F32, 8, 512)
    setupT = LT(RA, 0, F32, 6, N0)
    def mkviews(RX, RY):
        return dict(
            xT=LT(RX, 0, BF16, 32, 640), hT=LT(RX, 0, BF16, 32, 512),
            tmp4f=LT(RX, 32768, F32, 2, 512), tmp4b=LT(RX, 36864, BF16, 4, 512),
            tmp3f=LT(RX, 0, F32, 8, 512), tmp3b=LT(RX, 16384, BF16, 4, 512), tmpL=LT(RX, 0, F32, 4, 512),
            tmp1=LT(RY, 0, F32, 12, 640), mg=LT(RY, 0, BF16, 32, 512), x1T=LT(RY, 0, BF16, 32, 512))

    VIEWS = [mkviews(RB, RC), mkviews(RC, RB)]
    wr = LT(RW, 0, BF16, NW, 4096)
    ostage = LT(RM, 0, F32, 2, 512)
    lnt = LT(RM, 4096, F32, 3, 512)
    masks = LT(RM, 10240, BF16, 3, 512)
    ones = LT(RM, 13312, BF16, 1, 128)
    perm = LT(RM, 13568, F32, 1, 128)
    lnps = LT(RM, 14080, F32, 1, NL * 128)
    lnpa = LT(RM, 16128, F32, 1, NL * 128)
    cws = LT(RM, 18176, F32, 1, NL * 24)
    esk = LT(RM, 18560, F32, 1, NL * 32)
    vecs = LT(RM, 19072, F32, 1, 4)

    def PS(b):
        return ("ps", b)

    bank_rr = [0]

    def next_bank():
        b = bank_rr[0]
        bank_rr[0] = (b + 1) % 6
        return b

    slot_rr = [0]

    def load_slab(src_ap, kc):
        s = slot_rr[0]
        slot_rr[0] = (s + 1) % NW
        dst = wr.ap(s, 0, kc * 128)
        P.add("pool", lambda e, d=dst, a=src_ap: e.dma_start(out=d, in_=a), writes=[("w", s)], dma_key=("w", s))
        return s

    def wsl(s, kc):
        return wr.ap(s, kc * 128, (kc + 1) * 128)

    def mm(out_ap, lhsT, rhs, start, stop, reads, bank):
        P.add("pe", lambda e: e.matmul(out_ap, lhsT, rhs, start=start, stop=stop), reads=reads, writes=[PS(bank)])

    def act(out_ap, in_ap, func, reads, writes, bias=None, scale=None):
        kw = {}
        if bias is not None:
            kw["bias"] = bias
        if scale is not None:
            kw["scale"] = scale
        P.add("act", lambda e: e.activation(out_ap, in_ap, func, **kw), reads=reads, writes=writes)

    def tt(out_ap, in0, in1, op, reads, writes):
        P.add("dve", lambda e: e.tensor_tensor(out_ap, in0, in1, op), reads=reads, writes=writes)

    def ts(out_ap, in0, s1, s2, op0, op1, reads, writes):
        if s2 is None:
            P.add("dve", lambda e: e.tensor_scalar(out_ap, in0, s1, None, op0), reads=reads, writes=writes)
        else:
            P.add("dve", lambda e: e.tensor_scalar(out_ap, in0, s1, s2, op0, op1), reads=reads, writes=writes)

    def stt(out_ap, in0, sc, in1, op0, op1, reads, writes):
        P.add("dve", lambda e: e.scalar_tensor_tensor(out_ap, in0, sc, in1, op0, op1), reads=reads, writes=writes)

    def sp_load(dst_lt, r, c0, c1, src, key):
        P.add("sp", lambda e: e.dma_start(out=dst_lt.ap(r, c0, c1), in_=src), writes=dst_lt.g1(r, c0, c1), dma_key=key)

    sp_load(perm, 0, 0, 128, perm_d, "c0")
    sp_load(lnps, 0, 0, NL * 128, lnp, "c1")
    sp_load(cws, 0, 0, NL * 24, cwp, "c2")
    sp_load(esk, 0, 0, NL * 32, snk, "c3")
    sp_load(vecs, 0, 0, 4, vec, "c4")
    for i in range(3):
        P.add("pool", lambda e, i=i: e.dma_start(out=masks.ap(i, 0, 512), in_=cst[:, i * 512:(i + 1) * 512]),
              writes=masks.g1(i, 0, 512), dma_key=("m", i))
    P.add("dve", lambda e: e.memset(ones.ap(0, 0, 128), 1.0), writes=ones.g1(0, 0, 128))
    act(esk.ap(0, 0, NL * 32), esk.ap(0, 0, NL * 32), AF.Exp, esk.g1(0, 0, NL * 32), esk.g1(0, 0, NL * 32))
    P.add("dve", lambda e: e.tensor_scalar(lnpa.ap(0, 0, NL * 128), lnps.ap(0, 0, NL * 128), ALPHA, None, ALU.mult),
          reads=lnps.g1(0, 0, NL * 128), writes=lnpa.g1(0, 0, NL * 128))
    posi = setupT.ap3[:, 0, :].bitcast(I32)
    P.add("sp", lambda e: e.dma_start(out=posi, in_=posr), writes=setupT.g1(0, 0, N0), dma_key="c5")
    P.add("dve", lambda e: e.tensor_copy(setupT.ap(1, 0, N0), posi), reads=setupT.g1(0, 0, N0), writes=setupT.g1(1, 0, N0))
    fq = vecs.ap(0, 0, 1)
    sg = vecs.ap(0, 1, 2)
    flag = vecs.ap(0, 2, 3)
    vg = vecs.g1(0, 0, 4)
    epsb = vecs.ap(0, 3, 4)
    C1 = 6.28125
    C2 = TWO_PI - 6.28125
    SH = 1.0 - 1e-6

    def sT(r):
        return setupT.ap(r, 0, N0)

    def sG(r):
        return setupT.g1(r, 0, N0)

    ki_ap = setupT.ap3[:, 0, :].bitcast(I32)
    ts(sT(2), sT(1), fq, None, ALU.mult, None, sG(1) + vg, sG(2))
    ts(sT(1), sT(2), 1.0 / TWO_PI, None, ALU.mult, None, sG(2), sG(1))
    P.add("dve", lambda e: e.tensor_copy(ki_ap, sT(1)), reads=sG(1), writes=sG(0))
    P.add("dve", lambda e: e.tensor_copy(sT(1), ki_ap), reads=sG(0), writes=sG(1))
    stt(sT(3), sT(1), -C1, sT(2), ALU.mult, ALU.add, sG(1) + sG(2), sG(3))
    stt(sT(3), sT(1), -C2, sT(3), ALU.mult, ALU.add, sG(1) + sG(3), sG(3))

    def fold(r, tmp):
        ts(sT(tmp), sT(r), -math.pi, 1e9, ALU.add, ALU.mult, sG(r), sG(tmp))
        ts(sT(tmp), sT(tmp), 0.0, 1.0, ALU.max, ALU.min, sG(tmp), sG(tmp))
        stt(sT(r), sT(tmp), -TWO_PI, sT(r), ALU.mult, ALU.add, sG(tmp) + sG(r), sG(r))

    fold(3, 1)
    act(sT(4), sT(3), AF.Sin, sG(3), sG(4), scale=SH)
    ts(sT(4), sT(4), sg, None, ALU.mult, None, sG(4) + vg, sG(4))
    ts(sT(2), sT(3), 0.5 * math.pi, None, ALU.add, None, sG(3), sG(2))
    fold(2, 1)
    act(sT(5), sT(2), AF.Sin, sG(2), sG(5), scale=SH)
    P.add("sp", lambda e: e.dma_start(out=ropeC_d, in_=setupT.ap(5, 0, N0)), reads=setupT.g1(5, 0, N0), writes=[("ropeC",)], dma_key="c6")
    P.add("sp", lambda e: e.dma_start(out=ropeS_d, in_=setupT.ap(4, 0, N0)), reads=setupT.g1(4, 0, N0), writes=[("ropeS",)], dma_key="c7")

    out_store_keys = []

    tiles_all = []
    for l in range(NL):
        for (g0, T) in layer_tiles(HALO * (l + 1), N0):
            tiles_all.append((l, g0, T))

    def layer_io(l):
        xin = xin0 if l == 0 else xbufs[(l - 1) % 2]
        last = (l == NL - 1)
        xo = yout if last else xbufs[l % 2]
        xo_off = -S0 if last else 0
        return xin, xo, xo_off, last

    def emit_xT_load(ti):
        l_, g0_, T_ = tiles_all[ti]
        xin_ = layer_io(l_)[0]
        xin_v_ = xin_.rearrange("(c p) t -> p c t", p=128)
        xT_ = VIEWS[ti % 2]["xT"]
        TH_ = T_ + 128
        gi0_ = g0_ - 128
        for part in range(4):
            c0, c1 = part * 8, part * 8 + 8
            dst = xT_.aps(c0, c1, 0, TH_)
            src = xin_v_[:, c0:c1, gi0_:gi0_ + TH_]
            P.add("pool", lambda e, dst=dst, src=src: e.dma_start(out=dst, in_=src),
                  reads=[("xd", id(xin_), c, (gi0_ + k * 128) // 128) for c in range(c0, c1) for k in range(TH_ // 128)],
                  writes=xT_.gr(c0, c1, 0, TH_), dma_key=("xT", part))

    emit_xT_load(0)

    for ti, (l, g0, T) in enumerate(tiles_all):
        if True:
            xin, xo, xo_off, last = layer_io(l)
            g1c = l * 128
            b1c = l * 128 + 32
            g2c = l * 128 + 64
            b2c = l * 128 + 96
            V_ = VIEWS[ti % 2]
            xT = V_["xT"]; hT = V_["hT"]; tmp4f = V_["tmp4f"]; tmp4b = V_["tmp4b"]; tmp3f = V_["tmp3f"]
            tmp3b = V_["tmp3b"]; tmpL = V_["tmpL"]; tmp1 = V_["tmp1"]; mg = V_["mg"]; x1T = V_["x1T"]
            TH = T + 128
            gi0 = g0 - 128
            dsp = S0 - g0
            special = (0 <= dsp < T)
            NB = T // 128

            P.add("sp", lambda e, TH=TH, gi0=gi0: e.dma_start(out=rC.ap(0, 0, TH), in_=ropeC_d[:, gi0:gi0 + TH]),
                  reads=[("ropeC",)], writes=rC.g1(0, 0, TH), dma_key="rC")
            P.add("sp", lambda e, TH=TH, gi0=gi0: e.dma_start(out=rS.ap(0, 0, TH), in_=ropeS_d[:, gi0:gi0 + TH]),
                  reads=[("ropeS",)], writes=rS.g1(0, 0, TH), dma_key="rS")

            t1rr = [0]

            def t1slot():
                s = t1rr[0]
                t1rr[0] = (s + 1) % 12
                return s

            pend = []

            def flush():
                for f in pend:
                    f()
                del pend[:]

            def rope(bank, c0t, n, dst_lt, dst_r, dst_c0):
                a = t1slot(); b_ = t1slot(); c_ = t1slot()
                pb = psb[bank]
                act(tmp1.ap(a, 0, n), pb[:, 0:n], AF.Copy, [PS(bank)], tmp1.g1(a, 0, n))

                tmp1_ = tmp1

                def part2():
                    bk2 = next_bank()
                    rhs_ap = tmp1_.ap(a, 0, n)
                    P.add("pe", lambda e: e.matmul(psb[bk2][:, 0:n], perm.ap(0, 0, 128), rhs_ap, start=True, stop=True),
                          reads=perm.g1(0, 0, 128) + tmp1_.g1(a, 0, n), writes=[PS(bk2)])
                    tt(tmp1.ap(b_, 0, n), tmp1.ap(a, 0, n), rC.ap(0, c0t, c0t + n), ALU.mult,
                       tmp1.g1(a, 0, n) + rC.g1(0, c0t, c0t + n), tmp1.g1(b_, 0, n))
                    tt(tmp1.ap(c_, 0, n), psb[bk2][:, 0:n], rS.ap(0, c0t, c0t + n), ALU.mult,
                       [PS(bk2)] + rS.g1(0, c0t, c0t + n), tmp1.g1(c_, 0, n))
                    tt(dst_lt.ap(dst_r, dst_c0, dst_c0 + n), tmp1.ap(b_, 0, n), tmp1.ap(c_, 0, n), ALU.add,
                       tmp1.g1(b_, 0, n) + tmp1.g1(c_, 0, n), dst_lt.g1(dst_r, dst_c0, dst_c0 + n))
                pend.append(part2)

            wbase = l * N_WIN
            for r in range(16):
                s = load_slab(win[wbase + r], 32)
                bk = next_bank()
                for kc in range(32):
                    mm(psb[bk][:, 0:T], wsl(s, kc), xT.ap(kc, 128, TH), kc == 0, kc == 31,
                       [("w", s)] + xT.g1(kc, 128, TH), bk)
                flush()
                rope(bk, 128, T, qT, r, 0)
            for r in range(4):
                s = load_slab(win[wbase + 16 + r], 32)
                bk = next_bank()
                for kc in range(32):
                    mm(psb[bk][:, 0:T], wsl(s, kc), xT.ap(kc, 128, TH), kc == 0, kc == 31,
                       [("w", s)] + xT.g1(kc, 128, TH), bk)
                bkh = next_bank()
                for kc in range(32):
                    mm(psb[bkh][:, 0:128], wsl(s, kc), xT.ap(kc, 0, 128), kc == 0, kc == 31,
                       [("w", s)] + xT.g1(kc, 0, 128), bkh)
                flush()
                rope(bk, 128, T, kT, r, 128)
                rope(bkh, 0, 128, kT, r, 0)
            for j in range(4):
                s = load_slab(win[wbase + 44 + j], 32)
                for blk in range(NB + 1):
                    bk = next_bank()
                    po = psb[bk][:, 0:128]
                    for kc in range(32):
                        mm(po, xT.ap(kc, blk * 128, blk * 128 + 128), wsl(s, kc), kc == 0, kc == 31,
                           [("w", s)] + xT.g1(kc, blk * 128, blk * 128 + 128), bk)
                    if j == 0 and blk == 0:
                        flush()
                    act(Vt.ap(blk, j * 128, j * 128 + 128), po, AF.Copy, [PS(bk)], Vt.g1(blk, j * 128, j * 128 + 128))
            flush()
            abank = [0]

            pass

            ptrr = [0]

            def stage_a(n, g, half):
                p0, p1 = half * 64, half * 64 + 64
                pts = []
                for kb in range(2):
                    kc0 = (n + kb) * 128
                    bk = next_bank()
                    P.add("pe", lambda e, bk=bk, kc0=kc0:
                          e.matmul(psb[bk][:, 0:512], kT.ap(g, kc0, kc0 + 128, p0, p1),
                                   qT.aps(4 * g, 4 * g + 4, n * 128, n * 128 + 128, p0, p1), start=True, stop=True),
                          reads=kT.g1(g, kc0, kc0 + 128) + qT.gr(4 * g, 4 * g + 4, n * 128, n * 128 + 128),
                          writes=[PS(bk)])
                    pi = ptrr[0]
                    ptrr[0] = (pi + 1) % 4
                    act(PT.ap(pi, 0, 512), psb[bk][:, 0:512], AF.Exp, [PS(bk)], PT.g1(pi, 0, 512), scale=0.125)
                    if kb == 1:
                        mi = 1
                    else:
                        mi = 2 if (special and n * 128 == dsp) else 0
                    tt(PT.ap(pi, 0, 512), PT.ap(pi, 0, 512), masks.ap(mi, 0, 512), ALU.mult,
                       PT.g1(pi, 0, 512) + masks.g1(mi, 0, 512), PT.g1(pi, 0, 512))
                    pts.append(pi)
                return pts

            def stage_b(n, g, half, pts):
                p0, p1 = half * 64, half * 64 + 64
                bo = next_bank()
                for kb in range(2):
                    P.add("pe", lambda e, kb=kb, pi=pts[kb]:
                          e.matmul(psb[bo][:, 0:512], Vt.ap(n + kb, g * 128, g * 128 + 128), PT.ap(pi, 0, 512),
                                   start=(kb == 0), stop=(kb == 1)),
                          reads=Vt.g1(n + kb, g * 128, g * 128 + 128) + PT.g1(pts[kb], 0, 512), writes=[PS(bo)])
                bd = next_bank()
                for kb in range(2):
                    P.add("pe", lambda e, kb=kb, pi=pts[kb]:
                          e.matmul(psb[bd][:, 0:512], ones.ap(0, 0, 128), PT.ap(pi, 0, 512),
                                   start=(kb == 0), stop=(kb == 1)),
                          reads=ones.g1(0, 0, 128) + PT.g1(pts[kb], 0, 512), writes=[PS(bd)])
                rt = t1slot()
                for h in range(4):
                    hd = l * 32 + (8 * g + 4 * half + h)
                    act(tmp1.ap(rt, h * 128, h * 128 + 128, p0, p1), psb[bd][p0:p1, h * 128:h * 128 + 128], AF.Ln,
                        [PS(bd)] + esk.g1(0, hd, hd + 1), tmp1.g1(rt, h * 128, h * 128 + 128),
                        bias=esk.ap(0, hd, hd + 1, p0, p1), scale=1.0)
                act(tmp1.ap(rt, 0, 512, p0, p1), tmp1.ap(rt, 0, 512, p0, p1), AF.Exp, tmp1.g1(rt, 0, 512), tmp1.g1(rt, 0, 512), scale=-1.0)
                rec_ap = tmp1.ap3[p0:p1, rt, 0:512].rearrange("p (h q) -> p h q", q=128)
                P.add("dve", lambda e:
                      e.tensor_tensor(aT.aps(4 * g, 4 * g + 4, n * 128, n * 128 + 128, p0, p1),
                                      psb[bo][p0:p1, 0:512].rearrange("p (h q) -> p h q", q=128),
                                      rec_ap, ALU.mult),
                      reads=[PS(bo)] + tmp1.g1(rt, 0, 512),
                      writes=aT.gr(4 * g, 4 * g + 4, n * 128, n * 128 + 128))

            its = [(n, g, half) for n in range(NB) for g in range(4) for half in range(2)]
            astate = {"k": 0, "prev": None, "slab": 0}

            def attn_step():
                k = astate["k"]
                if k >= len(its):
                    return False
                it = its[k]
                pts = stage_a(*it)
                if astate["prev"] is not None:
                    stage_b(*astate["prev"])
                astate["prev"] = it + (pts,)
                astate["k"] = k + 1
                return True

            def attn_hook():
                astate["slab"] += 1
                target = (astate["slab"] * len(its) + 23) // 24
                while astate["k"] < target and attn_step():
                    pass

            for c in range(8):
                cwb = l * 24 + c * 3
                w0 = cws.ap(0, cwb, cwb + 1); w1 = cws.ap(0, cwb + 1, cwb + 2); w2 = cws.ap(0, cwb + 2, cwb + 3)
                cg = cws.g1(0, cwb, cwb + 3)
                s = load_slab(win[wbase + 20 + c * 3 + 0], 32)
                bka = next_bank()
                for kc in range(32):
                    mm(psb[bka][:, 0:T], wsl(s, kc), xT.ap(kc, 128, TH), kc == 0, kc == 31, [("w", s)] + xT.g1(kc, 128, TH), bka)
                for kc in range(32):
                    mm(psb[6][:, 0:2], wsl(s, kc), xT.ap(kc, 126, 128), kc == 0, kc == 31, [("w", s)] + xT.g1(kc, 126, 128), 6)
                flush()
                ta = t1slot(); tu = t1slot(); tb = t1slot(); tc_ = t1slot()
                act(tmp1.ap(ta, 2, 2 + T), psb[bka][:, 0:T], AF.Copy, [PS(bka)], tmp1.g1(ta, 2, 2 + T))
                if special and dsp == 0:
                    act(tmp1.ap(ta, 0, 2), psb[6][:, 0:2], AF.Identity, [PS(6)] + vg, tmp1.g1(ta, 0, 2), scale=flag)
                else:
                    act(tmp1.ap(ta, 0, 2), psb[6][:, 0:2], AF.Copy, [PS(6)], tmp1.g1(ta, 0, 2))
                    if special:
                        act(tmp1.ap(ta, dsp, dsp + 2), tmp1.ap(ta, dsp, dsp + 2), AF.Identity,
                            tmp1.g1(ta, dsp, dsp + 2) + vg, tmp1.g1(ta, dsp, dsp + 2), scale=flag)
                attn_hook()
                s = load_slab(win[wbase + 20 + c * 3 + 1], 32)
                bkb = next_bank()
                for kc in range(32):
                    mm(psb[bkb][:, 0:T], wsl(s, kc), xT.ap(kc, 128, TH), kc == 0, kc == 31, [("w", s)] + xT.g1(kc, 128, TH), bkb)
                for kc in range(32):
                    mm(psb[7][:, 0:2], wsl(s, kc), xT.ap(kc, 126, 128), kc == 0, kc == 31, [("w", s)] + xT.g1(kc, 126, 128), 7)
                tt(tmp1.ap(tu, 2, 2 + T), tmp1.ap(ta, 2, 2 + T), psb[bkb][:, 0:T], ALU.mult,
                   tmp1.g1(ta, 2, 2 + T) + [PS(bkb)], tmp1.g1(tu, 2, 2 + T))
                tt(tmp1.ap(tu, 0, 2), tmp1.ap(ta, 0, 2), psb[7][:, 0:2], ALU.mult,
                   tmp1.g1(ta, 0, 2) + [PS(7)], tmp1.g1(tu, 0, 2))
                ts(tmp1.ap(tb, 0, T), tmp1.ap(tu, 0, T), w0, None, ALU.mult, None, tmp1.g1(tu, 0, T) + cg, tmp1.g1(tb, 0, T))
                stt(tmp1.ap(tc_, 0, T), tmp1.ap(tu, 1, 1 + T), w1, tmp1.ap(tb, 0, T), ALU.mult, ALU.add,
                    tmp1.g1(tu, 1, 1 + T) + tmp1.g1(tb, 0, T) + cg, tmp1.g1(tc_, 0, T))
                stt(tmp1.ap(tb, 0, T), tmp1.ap(tu, 2, 2 + T), w2, tmp1.ap(tc_, 0, T), ALU.mult, ALU.add,
                    tmp1.g1(tu, 2, 2 + T) + tmp1.g1(tc_, 0, T) + cg, tmp1.g1(tb, 0, T))
                attn_hook()
                s = load_slab(win[wbase + 20 + c * 3 + 2], 32)
                bkc = next_bank()
                for kc in range(32):
                    mm(psb[bkc][:, 0:T], wsl(s, kc), xT.ap(kc, 128, TH), kc == 0, kc == 31, [("w", s)] + xT.g1(kc, 128, TH), bkc)
                tt(yconv.ap(c, 0, T), tmp1.ap(tb, 0, T), psb[bkc][:, 0:T], ALU.mult,
                   tmp1.g1(tb, 0, T) + [PS(bkc)], yconv.g1(c, 0, T))
                attn_hook()
            while attn_step():
                pass
            stage_b(*astate["prev"])
            t2rr = 0
            for m in range(32):
                s = load_slab(win[wbase + 48 + 2 * m], 32)
                bg = next_bank()
                for kc in range(32):
                    mm(psb[bg][:, 0:T], wsl(s, kc), xT.ap(kc, 128, TH), kc == 0, kc == 31, [("w", s)] + xT.g1(kc, 128, TH), bg)
                ia = t2rr; ib = t2rr + 1; ic = t2rr + 2; idd = t2rr + 3
                t2rr = (t2rr + 4) % 8
                act(tmp2.ap(ia, 0, T), psb[bg][:, 0:T], AF.Sigmoid, [PS(bg)], tmp2.g1(ia, 0, T))
                s = load_slab(win[wbase + 48 + 2 * m + 1], 32)
                bg2 = next_bank()
                for kc in range(32):
                    mm(psb[bg2][:, 0:T], wsl(s, kc), xT.ap(kc, 128, TH), kc == 0, kc == 31, [("w", s)] + xT.g1(kc, 128, TH), bg2)
                act(tmp2.ap(ib, 0, T), psb[bg2][:, 0:T], AF.Sigmoid, [PS(bg2)], tmp2.g1(ib, 0, T))
                s = load_slab(wbr[l * N_WBR + m], 24)
                ba = next_bank()
                for kc in range(16):
                    mm(psb[ba][:, 0:T], wsl(s, kc), aT.ap(kc, 0, T), kc == 0, kc == 15, [("w", s)] + aT.g1(kc, 0, T), ba)
                bc = next_bank()
                for kc in range(8):
                    mm(psb[bc][:, 0:T], wsl(s, 16 + kc), yconv.ap(kc, 0, T), kc == 0, kc == 7, [("w", s)] + yconv.g1(kc, 0, T), bc)
                tt(tmp2.ap(ic, 0, T), tmp2.ap(ia, 0, T), psb[ba][:, 0:T], ALU.mult, tmp2.g1(ia, 0, T) + [PS(ba)], tmp2.g1(ic, 0, T))
                tt(tmp2.ap(idd, 0, T), tmp2.ap(ib, 0, T), psb[bc][:, 0:T], ALU.mult, tmp2.g1(ib, 0, T) + [PS(bc)], tmp2.g1(idd, 0, T))
                tt(mg.ap(m, 0, T), tmp2.ap(ic, 0, T), tmp2.ap(idd, 0, T), ALU.add,
                   tmp2.g1(ic, 0, T) + tmp2.g1(idd, 0, T), mg.g1(m, 0, T))

            def stats(m, ybf_lt, ybf_r, ysq_r):
                act(ybf_lt.ap(ybf_r, 0, T), y.ap(m, 0, T), AF.Copy, y.g1(m, 0, T), ybf_lt.g1(ybf_r, 0, T))
                act(ybf_lt.ap(ysq_r, 0, T), y.ap(m, 0, T), AF.Square, y.g1(m, 0, T), ybf_lt.g1(ysq_r, 0, T))

                def part2():
                    mm(psb[6][:, 0:T], ones.ap(0, 0, 128), ybf_lt.ap(ybf_r, 0, T), m == 0, m == 31,
                       ones.g1(0, 0, 128) + ybf_lt.g1(ybf_r, 0, T), 6)
                    mm(psb[7][:, 0:T], ones.ap(0, 0, 128), ybf_lt.ap(ysq_r, 0, T), m == 0, m == 31,
                       ones.g1(0, 0, 128) + ybf_lt.g1(ysq_r, 0, T), 7)
                pend.append(part2)

            def ln_finish():
                ts(lnt.ap(2, 0, T), psb[6][:, 0:T], 1.0 / D, None, ALU.mult, None, [PS(6)], lnt.g1(2, 0, T))
                tt(lnt.ap(1, 0, T), lnt.ap(2, 0, T), lnt.ap(2, 0, T), ALU.mult, lnt.g1(2, 0, T), lnt.g1(1, 0, T))
                stt(lnt.ap(0, 0, T), psb[7][:, 0:T], 1.0 / D, lnt.ap(1, 0, T), ALU.mult, ALU.subtract,
                    [PS(7)] + lnt.g1(1, 0, T), lnt.g1(0, 0, T))
                act(lnt.ap(0, 0, T), lnt.ap(0, 0, T), AF.Sqrt, lnt.g1(0, 0, T) + vg, lnt.g1(0, 0, T), bias=epsb, scale=1.0)
                P.add("dve", lambda e: e.reciprocal(lnt.ap(0, 0, T), lnt.ap(0, 0, T)), reads=lnt.g1(0, 0, T), writes=lnt.g1(0, 0, T))
                stt(lnt.ap(1, 0, T), lnt.ap(2, 0, T), -1.0, lnt.ap(0, 0, T), ALU.mult, ALU.mult,
                    lnt.g1(2, 0, T) + lnt.g1(0, 0, T), lnt.g1(1, 0, T))

            for m in range(32):
                s = load_slab(wo[l * N_WO + m], 32)
                bk = next_bank()
                for kc in range(32):
                    mm(psb[bk][:, 0:T], wsl(s, kc), mg.ap(kc, 0, T), kc == 0, kc == 31, [("w", s)] + mg.g1(kc, 0, T), bk)
                flush()
                xs = m % 4
                xr_dst = tmp3f.ap(xs, 0, T)
                P.add("sp", lambda e, xr_dst=xr_dst, m=m, g0=g0, T=T, xin=xin:
                      e.dma_start(out=xr_dst, in_=xin[m * 128:(m + 1) * 128, g0:g0 + T]),
                      reads=[("xd", id(xin), m, (g0 + k * 128) // 128) for k in range(T // 128)],
                      writes=tmp3f.g1(xs, 0, T), dma_key=("xr", xs))
                stt(y.ap(m, 0, T), tmp3f.ap(xs, 0, T), ALPHA, psb[bk][:, 0:T], ALU.mult, ALU.add,
                    tmp3f.g1(xs, 0, T) + [PS(bk)], y.g1(m, 0, T))
                stats(m, tmp3b, (m % 2) * 2, (m % 2) * 2 + 1)
            flush()
            ln_finish()
            for m in range(32):
                ta = 4 + (m % 2) * 2
                tb = ta + 1
                tt(tmp3f.ap(ta, 0, T), y.ap(m, 0, T), lnt.ap(0, 0, T), ALU.mult, y.g1(m, 0, T) + lnt.g1(0, 0, T), tmp3f.g1(ta, 0, T))
                tt(tmp3f.ap(tb, 0, T), tmp3f.ap(ta, 0, T), lnt.ap(1, 0, T), ALU.add, tmp3f.g1(ta, 0, T) + lnt.g1(1, 0, T), tmp3f.g1(tb, 0, T))
                act(x1T.ap(m, 0, T), tmp3f.ap(tb, 0, T), AF.Identity, tmp3f.g1(tb, 0, T) + lnps.g1(0, g1c + m, g1c + m + 1) + lnps.g1(0, b1c + m, b1c + m + 1),
                    x1T.g1(m, 0, T), bias=lnps.ap(0, b1c + m, b1c + m + 1), scale=lnps.ap(0, g1c + m, g1c + m + 1))
                act(y.ap(m, 0, T), tmp3f.ap(tb, 0, T), AF.Identity, tmp3f.g1(tb, 0, T) + lnpa.g1(0, g1c + m, g1c + m + 1) + lnpa.g1(0, b1c + m, b1c + m + 1),
                    y.g1(m, 0, T), bias=lnpa.ap(0, b1c + m, b1c + m + 1), scale=lnpa.ap(0, g1c + m, g1c + m + 1))

            for gq in range(4):
                for j in range(32):
                    s = load_slab(wup[l * N_WUP + gq * 32 + j], 32)
                    bk = next_bank()
                    for kc in range(32):
                        mm(psb[bk][:, 0:T], wsl(s, kc), x1T.ap(kc, 0, T), kc == 0, kc == 31, [("w", s)] + x1T.g1(kc, 0, T), bk)
                    tr = j % 2
                    act(tmp4f.ap(tr, 0, T), psb[bk][:, 0:T], AF.Relu, [PS(bk)], tmp4f.g1(tr, 0, T))
                    tt(hT.ap(j, 0, T), tmp4f.ap(tr, 0, T), tmp4f.ap(tr, 0, T), ALU.mult, tmp4f.g1(tr, 0, T), hT.g1(j, 0, T))
                for m in range(32):
                    s = load_slab(wdn[l * N_WDN + gq * 32 + m], 32)
                    if gq == 3 and m == 6 and ti + 1 < len(tiles_all):
                        emit_xT_load(ti + 1)
                    bk = next_bank()
                    for kc in range(32):
                        mm(psb[bk][:, 0:T], wsl(s, kc), hT.ap(kc, 0, T), kc == 0, kc == 31, [("w", s)] + hT.g1(kc, 0, T), bk)
                    flush()
                    tt(y.ap(m, 0, T), y.ap(m, 0, T), psb[bk][:, 0:T], ALU.add, y.g1(m, 0, T) + [PS(bk)], y.g1(m, 0, T))
                    if gq == 3:
                        stats(m, tmp4b, (m % 2) * 2, (m % 2) * 2 + 1)
            flush()
            ln_finish()
            for m in range(32):
                ta = (m % 2) * 2
                tb = ta + 1
                os_ = m % 2
                tt(tmpL.ap(ta, 0, T), y.ap(m, 0, T), lnt.ap(0, 0, T), ALU.mult, y.g1(m, 0, T) + lnt.g1(0, 0, T), tmpL.g1(ta, 0, T))
                tt(tmpL.ap(tb, 0, T), tmpL.ap(ta, 0, T), lnt.ap(1, 0, T), ALU.add, tmpL.g1(ta, 0, T) + lnt.g1(1, 0, T), tmpL.g1(tb, 0, T))
                act(ostage.ap(os_, 0, T), tmpL.ap(tb, 0, T), AF.Identity,
                    tmpL.g1(tb, 0, T) + lnps.g1(0, g2c + m, g2c + m + 1) + lnps.g1(0, b2c + m, b2c + m + 1),
                    ostage.g1(os_, 0, T), bias=lnps.ap(0, b2c + m, b2c + m + 1), scale=lnps.ap(0, g2c + m, g2c + m + 1))
                lo = g0 + xo_off
                c_lo = 0
                if lo < 0:
                    c_lo = -lo
                if c_lo >= T:
                    continue
                wres = [("xd", id(xo), m, (g0 + k * 128) // 128) for k in range(T // 128)]
                P.add("sp", lambda e, os_=os_, m=m, lo=lo, c_lo=c_lo, T=T, xo=xo:
                      e.dma_start(out=xo[m * 128:(m + 1) * 128, lo + c_lo:lo + T], in_=ostage.ap(os_, c_lo, T)),
                      reads=ostage.g1(os_, 0, T), writes=wres, dma_key=("os", os_))
                if last:
                    out_store_keys.extend(wres)

    P.add("sp", None, reads=list(dict.fromkeys(out_store_keys)))
    P.emit(nc, stack)
    stack.close()
    return nc


Q0, K0, V0, CB0, CC0, CH0, GA0, GC0 = 0, 2048, 2304, 2560, 3584, 4608, 5632, 9728


def _head_order():
    rows = []
    for r in range(16):
        g, i = r // 4, r % 4
        rows.append((8 * g + i, 8 * g + 4 + i))
    return rows


def _win_cols():
    cols = []
    for (ha, hb) in _head_order():
        cols.append(np.concatenate([Q0 + ha * 64 + np.arange(64), Q0 + hb * 64 + np.arange(64)]))
    for g in range(4):
        k = K0 + g * 64 + np.arange(64)
        cols.append(np.concatenate([k, k]))
    for c in range(8):
        cols.append(CC0 + c * 128 + np.arange(128))
        cols.append(CH0 + c * 128 + np.arange(128))
        cols.append(CB0 + c * 128 + np.arange(128))
    for j in range(4):
        v = V0 + j * 64 + np.arange(64)
        cols.append(np.concatenate([v, v]))
    for m in range(32):
        cols.append(GA0 + m * 128 + np.arange(128))
        cols.append(GC0 + m * 128 + np.arange(128))
    return np.concatenate(cols)


def _slabs(W, cols=None, rows=None):
    if rows is not None:
        W = W[rows]
    if cols is not None:
        W = W[:, cols]
    K, N = W.shape
    return np.ascontiguousarray(W.reshape(K // 128, 128, N // 128, 128).transpose(2, 1, 0, 3)).reshape(N // 128, 128, K)


def _prep_layer(w_in, w_br_attn, w_br_conv, w_o, w_up, w_down):
    win = _slabs(w_in, cols=_win_cols())
    arow = np.concatenate([np.concatenate([ha * 64 + np.arange(64), hb * 64 + np.arange(64)]) for (ha, hb) in _head_order()])
    wbr = np.concatenate([_slabs(w_br_attn, rows=arow), _slabs(w_br_conv)], axis=2)
    wo = _slabs(w_o)
    wup = _slabs(w_up)
    wd = w_down.reshape(4, 32, 128, 32, 128).transpose(0, 3, 2, 1, 4)
    wdn = np.ascontiguousarray(wd).reshape(128, 128, 4096)
    return win, wbr, wo, wup, wdn


def _consts():
    j = np.arange(128)[:, None]
    i = np.arange(128)[None, :]
    mp = (j > i).astype(np.float32)
    mc = (j <= i).astype(np.float32)
    perm = np.zeros((128, 128), np.float32)
    fq = np.zeros(128, np.float32)
    sg = np.zeros(128, np.float32)
    inv_freq = (ROPE_THETA ** (-np.arange(0, 16, 2, dtype=np.float32) / np.float32(16))).astype(np.float32)
    for m in range(128):
        d = m % 64
        if d < 8:
            perm[m + 8, m] = 1.0
            fq[m] = inv_freq[d]
            sg[m] = -1.0
        elif d < 16:
            perm[m - 8, m] = 1.0
            fq[m] = inv_freq[d - 8]
            sg[m] = 1.0
    return np.tile(mp, (1, 4)), np.tile(mc, (1, 4)), perm, fq, sg


_PROG_CACHE = {}


def _get_prog(NL):
    if NL not in _PROG_CACHE:
        _PROG_CACHE[NL] = build_program(NL)
    return _PROG_CACHE[NL]


def _small_params(layers, conv_w, attn_sinks, ln1_g, ln1_b, ln2_g, ln2_b):
    hord = np.array([h for pair in _head_order() for h in pair])
    lnp = np.concatenate([np.concatenate([a[l].reshape(32, 128).T for a in (ln1_g, ln1_b, ln2_g, ln2_b)], axis=1) for l in layers], axis=1)
    cwp = np.concatenate([conv_w[l].reshape(3, 8, 128).transpose(2, 1, 0).reshape(128, 24) for l in layers], axis=1)
    snk = np.concatenate([np.tile(attn_sinks[l][None, :], (128, 1)) for l in layers], axis=1)
    return (np.ascontiguousarray(lnp, dtype=np.float32), np.ascontiguousarray(cwp, dtype=np.float32),
            np.ascontiguousarray(snk, dtype=np.float32))


def kernel(x, positions, w_in, conv_w, attn_sinks, w_br_attn, w_br_conv, w_o,
           ln1_g, ln1_b, w_up, w_down, ln2_g, ln2_b):
    x = np.asarray(x); positions = np.asarray(positions)
    B, S, _ = x.shape
    cps = NCORES // B
    mp4, mc4, perm, fq, sg = _consts()
    zeros4 = np.zeros_like(mp4)

    def run_layers(xT_full, layers):
        NL = len(layers)
        N0 = TOK_CORE + HALO * NL
        nc = _get_prog(NL)
        preps = [_prep_layer(np.asarray(w_in[l]), np.asarray(w_br_attn[l]), np.asarray(w_br_conv[l]),
                             np.asarray(w_o[l]), np.asarray(w_up[l]), np.asarray(w_down[l])) for l in layers]
        win = np.concatenate([p[0] for p in preps], axis=0)
        wbr = np.concatenate([p[1] for p in preps], axis=0)
        wo = np.concatenate([p[2] for p in preps], axis=0)
        wup = np.concatenate([p[3] for p in preps], axis=0)
        wdn = np.concatenate([p[4] for p in preps], axis=0)
        del preps
        lnp, cwp, snk = _small_params(layers, np.asarray(conv_w), np.asarray(attn_sinks), np.asarray(ln1_g),
                                      np.asarray(ln1_b), np.asarray(ln2_g), np.asarray(ln2_b))
        in_maps = []
        for c in range(NCORES):
            b, q = c // cps, c % cps
            s0 = q * TOK_CORE
            h = HALO * NL
            xin = np.zeros((D, N0), np.float32)
            pos = np.zeros((N0,), np.int32)
            lo = max(0, s0 - h)
            xin[:, h - (s0 - lo):] = xT_full[b][:, lo:s0 + TOK_CORE]
            pos[h - (s0 - lo):] = positions[b, lo:s0 + TOK_CORE]
            first = (q == 0)
            cst = np.concatenate([mp4, mc4, zeros4 if first else mp4], axis=1).astype(np.float32)
            vec = np.stack([fq, sg, np.full(128, 0.0 if first else 1.0, np.float32), np.full(128, LN_EPS, np.float32)], axis=1)
            in_maps.append({
                "xin": xin, "posr": np.ascontiguousarray(np.tile(pos[None, :], (128, 1))),
                "win": win, "wbr": wbr, "wo": wo, "wup": wup, "wdn": wdn,
                "lnp": lnp, "cwp": cwp, "snk": snk, "cst": cst, "perm": perm,
                "vec": np.ascontiguousarray(vec, dtype=np.float32),
            })
        res = run_bass_kernel_spmd(nc, in_maps, core_ids=list(range(NCORES)))
        out = np.empty_like(xT_full)
        for c in range(NCORES):
            b, q = c // cps, c % cps
            out[b][:, q * TOK_CORE:(q + 1) * TOK_CORE] = res.results[c]["yout"]
        return out

    xT_full = np.ascontiguousarray(np.transpose(x, (0, 2, 1)))
    if FUSED:
        xT_full = run_layers(xT_full, list(range(DEPTH)))
    else:
        for l in range(DEPTH):
            xT_full = run_layers(xT_full, [l])
    return np.ascontiguousarray(np.transpose(xT_full, (0, 2, 1))).astype(np.float32)
```
